# Optimizing a Trainium2 kernel written in Bass

```python
import math
import jax, jax.numpy as jnp
from jax import lax
import numpy as np

D_MODEL = 1024
BATCH = 8
SEQ = 4096
DEPTH = 1

A_HEADS = 8
A_HEAD_DIM = 64
A_WIDTH = A_HEADS * A_HEAD_DIM
MOBA_BLOCK = 256
MOBA_TOPK = 3
MOBA_Q_CHUNK = 16
REL_BUCKETS = 32
REL_MAX_DIST = 128
B_HEADS = 8
QK_NOPE = 64
QK_ROPE = 32
V_HEAD = 64
B_WIDTH = B_HEADS * V_HEAD
Q_LORA = 384
KV_LORA = 256
ROPE_THETA = 10000.0
MLA_Q_BLOCK = 128
D_FF = 4 * D_MODEL
EPS = 1e-6
NEG = -1e30
IN_SPLITS = (A_WIDTH, A_WIDTH, A_WIDTH, Q_LORA, KV_LORA, QK_ROPE, D_MODEL, D_MODEL)
IN_COLS = sum(IN_SPLITS)

kernel_name = "hybrid_moba_mla_gated_block"


def rmsnorm(x, g):
    xf = x.astype(jnp.float32)
    y = xf * lax.rsqrt(jnp.mean(xf * xf, axis=-1, keepdims=True) + EPS)
    return (y * g.astype(jnp.float32)).astype(x.dtype)


def rope(x, pos):
    half = QK_ROPE // 2
    inv_freq = ROPE_THETA ** (-jnp.arange(half, dtype=jnp.float32) / half)
    ang = pos.astype(jnp.float32)[:, None] * inv_freq[None, :]
    cos = jnp.cos(ang)[None, :, None, :]
    sin = jnp.sin(ang)[None, :, None, :]
    xf = x.astype(jnp.float32)
    x1, x2 = xf[..., :half], xf[..., half:]
    return jnp.concatenate([x1 * cos - x2 * sin, x2 * cos + x1 * sin], axis=-1).astype(x.dtype)


def t5_bucket(dist):
    dist = jnp.maximum(dist, 0)
    max_exact = REL_BUCKETS // 2
    d = jnp.maximum(dist, 1).astype(jnp.float32)
    large = max_exact + (jnp.log(d / max_exact) / math.log(REL_MAX_DIST / max_exact)
                         * (REL_BUCKETS - max_exact)).astype(jnp.int32)
    large = jnp.minimum(large, REL_BUCKETS - 1)
    return jnp.where(dist < max_exact, dist, large)


def moba_attention(q, k, v, rel_bias):
    B, S, H, D = q.shape
    scale = D ** -0.5
    nblk = -(-S // MOBA_BLOCK)
    pad = nblk * MOBA_BLOCK - S
    kb = jnp.pad(k, ((0, 0), (0, pad), (0, 0), (0, 0))).reshape(B, nblk, MOBA_BLOCK, H, D).transpose(0, 3, 1, 2, 4)
    vb = jnp.pad(v, ((0, 0), (0, pad), (0, 0), (0, 0))).reshape(B, nblk, MOBA_BLOCK, H, D).transpose(0, 3, 1, 2, 4)
    k_mean = jnp.mean(kb, axis=3)
    pos = jnp.arange(S)
    q_blk = pos // MOBA_BLOCK
    gate = jnp.einsum('bshd,bhnd->bhsn', q, k_mean).astype(jnp.float32)
    past = jnp.arange(nblk)[None, :] < q_blk[:, None]
    gate = jnp.where(past[None, None], gate, NEG)
    n_sel = min(MOBA_TOPK, nblk)
    _, sel = lax.top_k(gate, n_sel)
    sel_valid = jnp.arange(n_sel)[None, :] < q_blk[:, None]
    bias_table = rel_bias.T
    nq = S // MOBA_Q_CHUNK
    QC = MOBA_Q_CHUNK
    qs = q.reshape(B, nq, QC, H, D).swapaxes(0, 1)
    sels = sel.reshape(B, H, nq, QC, n_sel).transpose(2, 0, 1, 3, 4)
    valids = sel_valid.reshape(nq, QC, n_sel)
    bi = jnp.arange(B)[:, None, None, None]
    hi = jnp.arange(H)[None, :, None, None]
    offs = jnp.arange(MOBA_BLOCK)

    def chunk_fn(args):
        c, qc, sc, vc = args
        q_pos = c * QC + jnp.arange(QC)
        own = (c * QC) // MOBA_BLOCK
        k_own = lax.dynamic_index_in_dim(kb, own, axis=2, keepdims=False)
        v_own = lax.dynamic_index_in_dim(vb, own, axis=2, keepdims=False)
        k_g = kb[bi, hi, sc]
        v_g = vb[bi, hi, sc]
        kpos_past = sc[..., None] * MOBA_BLOCK + offs
        bias_past = bias_table[hi[..., None], t5_bucket(q_pos[:, None, None] - kpos_past)]
        logit_past = jnp.einsum('bqhd,bhqrkd->bhqrk', qc, k_g).astype(jnp.float32) * scale + bias_past
        logit_past = jnp.where(vc[None, None, :, :, None], logit_past, NEG)
        kpos_own = own * MOBA_BLOCK + offs
        bias_own = bias_table[:, t5_bucket(q_pos[:, None] - kpos_own[None, :])]
        logit_own = jnp.einsum('bqhd,bhkd->bhqk', qc, k_own).astype(jnp.float32) * scale + bias_own
        logit_own = jnp.where((kpos_own[None, :] <= q_pos[:, None])[None, None], logit_own, NEG)
        logits = jnp.concatenate([logit_past.reshape(B, H, QC, n_sel * MOBA_BLOCK), logit_own], axis=-1)
        p = jax.nn.softmax(logits, axis=-1).astype(v.dtype)
        p_past = p[..., :n_sel * MOBA_BLOCK].reshape(B, H, QC, n_sel, MOBA_BLOCK)
        p_own = p[..., n_sel * MOBA_BLOCK:]
        return (jnp.einsum('bhqrk,bhqrkd->bqhd', p_past, v_g)
                + jnp.einsum('bhqk,bhkd->bqhd', p_own, v_own))

    out = lax.map(chunk_fn, (jnp.arange(nq), qs, sels, valids))
    return out.swapaxes(0, 1).reshape(B, S, H * D)


def mla_attention(q_nope, q_rope, k_nope, k_rope, v):
    B, S, H, _ = q_nope.shape
    scale = (QK_NOPE + QK_ROPE) ** -0.5
    nqb = S // MLA_Q_BLOCK
    k_pos = jnp.arange(S)
    qn = q_nope.reshape(B, nqb, MLA_Q_BLOCK, H, QK_NOPE).swapaxes(0, 1)
    qr = q_rope.reshape(B, nqb, MLA_Q_BLOCK, H, QK_ROPE).swapaxes(0, 1)

    def block_fn(args):
        i, qn_b, qr_b = args
        q_pos = i * MLA_Q_BLOCK + jnp.arange(MLA_Q_BLOCK)
        logits = (jnp.einsum('bqhd,bkhd->bhqk', qn_b, k_nope)
                  + jnp.einsum('bqhd,bkd->bhqk', qr_b, k_rope)).astype(jnp.float32) * scale
        mask = k_pos[None, :] <= q_pos[:, None]
        logits = jnp.where(mask[None, None], logits, NEG)
        p = jax.nn.softmax(logits, axis=-1).astype(v.dtype)
        return jnp.einsum('bhqk,bkhd->bqhd', p, v)

    out = lax.map(block_fn, (jnp.arange(nqb), qn, qr))
    return out.swapaxes(0, 1).reshape(B, S, H * V_HEAD)


def setup_inputs(seed: int = 0) -> dict:
    key = jax.random.key(seed)
    ks = jax.random.split(key, 17)

    def w(k, shape, fan_in):
        return jax.random.normal(k, shape, jnp.float32) * fan_in ** -0.5

    def gain(k, shape):
        return 1.0 + 0.05 * jax.random.normal(k, shape, jnp.float32)

    L = DEPTH
    return {
        'x': jax.random.normal(ks[0], (BATCH, SEQ, D_MODEL), jnp.float32),
        'w_in': w(ks[1], (L, D_MODEL, IN_COLS), D_MODEL),
        'rel_bias': 0.5 * jax.random.normal(ks[2], (REL_BUCKETS, A_HEADS), jnp.float32),
        'mla_q_norm': gain(ks[3], (L, Q_LORA)),
        'w_uq': w(ks[4], (L, Q_LORA, B_HEADS * (QK_NOPE + QK_ROPE)), Q_LORA),
        'mla_kv_norm': gain(ks[5], (L, KV_LORA)),
        'w_uk': w(ks[6], (L, KV_LORA, B_HEADS * QK_NOPE), KV_LORA),
        'w_uv': w(ks[7], (L, KV_LORA, B_HEADS * V_HEAD), KV_LORA),
        'w_proj_a': w(ks[8], (L, A_WIDTH, D_MODEL), A_WIDTH),
        'w_proj_b': w(ks[9], (L, B_WIDTH, D_MODEL), B_WIDTH),
        'w_out': w(ks[10], (L, D_MODEL, D_MODEL), D_MODEL),
        'norm_attn': gain(ks[11], (L, D_MODEL)),
        'norm_mlp': gain(ks[12], (L, D_MODEL)),
        'w_mlp_up': w(ks[13], (L, D_MODEL, D_FF), D_MODEL),
        'w_mlp_down': w(ks[14], (L, D_FF, D_MODEL), D_FF),
        'norm_final': gain(ks[15], (D_MODEL,)),
    }


def reference(x, w_in, rel_bias, mla_q_norm, w_uq, mla_kv_norm, w_uk, w_uv, w_proj_a, w_proj_b,
              w_out, norm_attn, norm_mlp, w_mlp_up, w_mlp_down, norm_final):
    B, S, _ = x.shape
    pos = jnp.arange(S)
    split_pts = list(np.cumsum(IN_SPLITS)[:-1])
    h = x
    for l in range(DEPTH):
        n = rmsnorm(h, norm_attn[l])
        proj = n @ w_in[l]
        qa, ka, va, cq, ckv, kr, ga, gb = jnp.split(proj, split_pts, axis=-1)
        oa = moba_attention(qa.reshape(B, S, A_HEADS, A_HEAD_DIM),
                            ka.reshape(B, S, A_HEADS, A_HEAD_DIM),
                            va.reshape(B, S, A_HEADS, A_HEAD_DIM), rel_bias)
        cq = rmsnorm(cq, mla_q_norm[l])
        q = (cq @ w_uq[l]).reshape(B, S, B_HEADS, QK_NOPE + QK_ROPE)
        q_nope, q_rope = q[..., :QK_NOPE], rope(q[..., QK_NOPE:], pos)
        ckv = rmsnorm(ckv, mla_kv_norm[l])
        k_nope = (ckv @ w_uk[l]).reshape(B, S, B_HEADS, QK_NOPE)
        v_b = (ckv @ w_uv[l]).reshape(B, S, B_HEADS, V_HEAD)
        k_rope = rope(kr[:, :, None, :], pos)[:, :, 0, :]
        ob = mla_attention(q_nope, q_rope, k_nope, k_rope, v_b)
        merged = jax.nn.sigmoid(ga) * (oa @ w_proj_a[l]) + jax.nn.sigmoid(gb) * (ob @ w_proj_b[l])
        h = h + merged @ w_out[l]
        m = rmsnorm(h, norm_mlp[l])
        h = h + jnp.square(jax.nn.relu(m @ w_mlp_up[l])) @ w_mlp_down[l]
    return rmsnorm(h, norm_final)
```

```python
import math
import numpy as np
import concourse.bass as bass
import concourse.mybir as mybir
from concourse.bass_utils import run_bass_kernel_spmd

F32 = mybir.dt.float32
BF16 = mybir.dt.bfloat16
ALU = mybir.AluOpType
AF = mybir.ActivationFunctionType
AX = mybir.AxisListType

SEQ = 4096
TC = 512
NCH = SEQ // TC
NEG = -30000.0
EPS = 1e-6
RING = 8
SL_Q, SL_K, SL_V, SL_CQ, SL_CKV, SL_KR, SL_UV, SL_HEAD = 0, 4, 8, 12, 15, 17, 19, 20
SL_G, SL_P, SL_O, SL_UP, SL_DN = 28, 44, 52, 60, 92
NSLAB = 124


class Task:
    __slots__ = ("eng", "fn", "deps", "sem", "val", "ndma", "has_waiter")

    def __init__(self, eng, fn, ndma):
        self.eng, self.fn, self.ndma = eng, fn, ndma
        self.deps, self.sem, self.val, self.has_waiter = [], None, None, False


class Sched:
    ENGS = ("pe", "act", "dve", "pool", "sp")

    def __init__(self):
        self.prog = {e: [] for e in self.ENGS}
        self.last_w, self.readers, self.chan_count, self.chan_last = {}, {}, {}, {}

    def add(self, eng, fn, reads=(), writes=(), chan=None, ndma=0, extra=()):
        t = Task(eng, fn, ndma)
        if chan is not None:
            c = self.chan_count.get(chan, 0) + 16 * ndma
            self.chan_count[chan] = c
            t.sem, t.val = ("chan", chan), c
            self.chan_last[chan] = t
        deps = {}
        for k in reads:
            w = self.last_w.get(k)
            if w is not None:
                deps[id(w)] = (w, "raw")
        for k in writes:
            w = self.last_w.get(k)
            if w is not None and id(w) not in deps:
                deps[id(w)] = (w, "waw")
            for r in self.readers.get(k, ()):
                if id(r) not in deps:
                    deps[id(r)] = (r, "war")
        for w in extra:
            deps[id(w)] = (w, "raw")
        for w, kind in deps.values():
            if w is t:
                continue
            if w.ndma == 0 and ndma == 0 and w.eng == eng and eng == "pe":
                continue
            t.deps.append(w)
            w.has_waiter = True
        for k in reads:
            self.readers.setdefault(k, []).append(t)
        for k in writes:
            self.last_w[k] = t
            self.readers[k] = []
        self.prog[eng].append(t)
        return t

    def barrier(self):
        first = [self.prog[e][-1] for e in self.ENGS if self.prog[e]]
        ex = first + list(self.chan_last.values())
        for e in self.ENGS:
            self.add(e, lambda h: h.nop(), extra=ex)

    def assign(self):
        for e in self.ENGS:
            cnt = 0
            for t in self.prog[e]:
                if t.ndma == 0 and t.has_waiter:
                    cnt += 1
                    t.sem, t.val = ("eng", e), cnt

    def semnames(self):
        s = set(("eng", e) for e in self.ENGS)
        for e in self.ENGS:
            for t in self.prog[e]:
                if t.sem is not None:
                    s.add(t.sem)
        return sorted(s)

    def emit_engine(self, e, h, sems):
        known = {}
        for t in self.prog[e]:
            need = {}
            for d in t.deps:
                if d.val > need.get(d.sem, 0):
                    need[d.sem] = d.val
            for s, v in need.items():
                if known.get(s, 0) >= v:
                    continue
                h.wait_ge(sems[s], v)
                known[s] = v
            r = t.fn(h)
            if t.ndma > 0:
                for ins in r:
                    ins.then_inc(sems[t.sem], 16)
            elif t.has_waiter:
                r.then_inc(sems[t.sem], 1)


DEBUG = False


def build(seq):
    nc = bass.Bass("TRN2", target_bir_lowering=False)
    din = lambda n, s: nc.dram_tensor(n, s, F32, kind="ExternalInput").ap()
    xT = din("xT", [1024, SEQ])
    wsrc = din("wsrc", [NSLAB, 128, 1024])
    c_gains = din("c_gains", [128, 32])
    c_tri = din("c_tri", [128, 128])
    c_E = din("c_E", [16, SEQ])
    c_caus = din("c_caus", [128, 256])
    c_Tm = din("c_Tm", [128, 8 * 256])
    c_b31 = din("c_b31", [128, 8])
    c_cos = din("c_cos", [32, SEQ])
    c_sin = din("c_sin", [32, SEQ])
    outT = nc.dram_tensor("outT", [1024, SEQ], F32, kind="ExternalOutput").ap()
    wscr = nc.dram_tensor("wscr", [NSLAB, 128, 1024], BF16, kind="Internal").ap()
    if DEBUG:
        dbg_oa = nc.dram_tensor("dbg_oa", [128, 4 * SEQ], BF16, kind="ExternalOutput").ap()
        dbg_ob = nc.dram_tensor("dbg_ob", [128, 4 * SEQ], BF16, kind="ExternalOutput").ap()
        dbg_h = nc.dram_tensor("dbg_h", [128, 8, SEQ], F32, kind="ExternalOutput").ap()
        dbg_h2 = nc.dram_tensor("dbg_h2", [128, 8, SEQ], F32, kind="ExternalOutput").ap()
        dbg_mg = nc.dram_tensor("dbg_mg", [128, 8, SEQ], BF16, kind="ExternalOutput").ap()
    xT3 = xT.rearrange("(k p) t -> p k t", p=128)
    outT3 = outT.rearrange("(k p) t -> p k t", p=128)

    S = Sched()
    rec = [] if seq is None else None

    SB_BASE = 16512 + 64
    top = [SB_BASE]
    cnt = [0]

    def sb(shape, dt):
        n = 1
        for s_ in shape[1:]:
            n *= s_
        nbytes = n * (4 if dt == F32 else 2)
        nbytes = (nbytes + 63) // 64 * 64
        off = top[0]
        top[0] += nbytes
        assert top[0] <= 229376, ("SBUF overflow", top[0])
        cnt[0] += 1
        return nc.alloc_sbuf_tensor_at(f"t{cnt[0]}", shape, dt, offset=off)

    psd = [nc.alloc_psum_tensor(f"psd{i}", [128, 1024], F32) for i in range(4)]
    ps = [psd[b // 2][:, (b % 2) * 512:(b % 2 + 1) * 512] for b in range(8)]
    psT = psd[3].bitcast(BF16)[:, 1024:2048]
    rr_b = [0]

    def psb():
        rr_b[0] ^= 1
        return 6 + rr_b[0]
    rr_g = [0]
    rr_m = [0]

    def psg():
        rr_g[0] = (rr_g[0] + 1) % 6
        return rr_g[0]

    def psm():
        return 6

    def T(eng, fn, r=(), w=(), **kw):
        return S.add(eng, fn, reads=r, writes=w, **kw)

    def mm(out, lhsT, rhs, r, w, start=True, stop=True):
        T("pe", lambda e: e.matmul(out, lhsT=lhsT, rhs=rhs, start=start, stop=stop), r, w)

    def actf(out, in_, func, r, w, scale=1.0, bias=None):
        if bias is None:
            T("act", lambda e: e.activation(out=out, in_=in_, func=func, scale=scale), r, w)
        else:
            T("act", lambda e: e.activation(out=out, in_=in_, func=func, scale=scale, bias=bias), r, w)

    def tt(eng, out, in0, in1, op, r, w):
        T(eng, lambda e: e.tensor_tensor(out=out, in0=in0, in1=in1, op=op), r, w)

    def ts(eng, out, in0, s1, op0, r, w, s2=None, op1=None):
        if op1 is None:
            T(eng, lambda e: e.tensor_scalar(out=out, in0=in0, scalar1=s1, scalar2=None, op0=op0), r, w)
        else:
            T(eng, lambda e: e.tensor_scalar(out=out, in0=in0, scalar1=s1, scalar2=s2, op0=op0, op1=op1), r, w)

    def stt(eng, out, in0, sc, in1, op0, op1, r, w):
        T(eng, lambda e: e.scalar_tensor_tensor(out=out, in0=in0, scalar=sc, in1=in1, op0=op0, op1=op1), r, w)

    def cp(eng, out, in_, r, w):
        if eng == "act":
            T("act", lambda e: e.activation(out=out, in_=in_, func=AF.Copy), r, w)
        else:
            T(eng, lambda e: e.tensor_copy(out=out, in_=in_), r, w)

    def mset(eng, ap, v, w):
        T(eng, lambda e: e.memset(ap, v), (), w)

    def dma(eng, out, in_, r, w, chan):
        T(eng, lambda e: [e.dma_start(out=out, in_=in_)], r, w, chan=chan, ndma=1)

    ring = [sb([128, 1024], BF16) for _ in range(RING)]
    oaT = sb([128, 4, SEQ], BF16)
    ident = sb([128, 128], BF16)
    ones = sb([128, 128], BF16)
    tri = sb([128, 128], BF16)
    gt = sb([128, 32], F32)
    gfin = gt[:, 21:29]
    att_base = top[0]
    Ptp = [sb([128, 2 * TC], BF16) for _ in range(4)]
    ofs = [sb([128, TC], F32) for _ in range(3)]
    DEF = []
    itc = [0]
    ofn = [0]

    def run_deferred(force=False):
        while DEF and (force or DEF[0][0] <= itc[0]):
            DEF.pop(0)[1]()
    rrfs = [sb([128, TC], BF16) for _ in range(3)]
    free_slots = [0, 1, 2]

    def slot():
        return free_slots.pop(0)

    def release(sl_):
        free_slots.append(sl_)
    stg = [sb([128, TC], BF16) for _ in range(2)]
    glob_top = top[0]

    class WM:
        pos = 0
        issued = 0

        def issue(self, i):
            slot = i % RING
            sl = seq[i]
            dma("sp", ring[slot][:], wscr[sl], [("scr", sl)], [("ring", slot)], f"ring{slot}")

        def get(self, sl):
            i = self.pos
            self.pos += 1
            if seq is None:
                rec.append(sl)
                slot = i % RING
                dma("sp", ring[slot][:], wscr[sl], [("scr", sl)], [("ring", slot)], f"ring{slot}")
                return slot
            assert seq[i] == sl, (i, seq[i], sl)
            while self.issued < min(len(seq), i + RING - 3):
                self.issue(self.issued)
                self.issued += 1
            return i % RING

    W = WM()

    mset("pool", ident[:], 1.0, ["ident"])
    T("pool", lambda e: e.affine_select(out=ident[:], in_=ident[:], pattern=[[-1, 128]], compare_op=ALU.is_equal,
                                        fill=0.0, base=0, channel_multiplier=1), ["ident"], ["ident"])
    mset("pool", ones[:], 1.0, ["ones"])
    dma("pool", tri[:], c_tri, [], ["tri"], "c0")
    dma("sp", gt[:], c_gains, [], ["gfin", "gains"], "c1")

    class WPrep:
        order = None
        pos = 0

        def step(self, n):
            for _ in range(n):
                if self.pos >= len(self.order):
                    return
                s_ = self.order[self.pos]
                ch = self.pos % 8
                self.pos += 1
                dma("pool", wscr[s_], wsrc[s_], [], [("scr", s_), ("wpch", ch)], f"wp{ch}")

    WP = WPrep()
    if seq is None:
        WP.order = list(range(NSLAB))
    else:
        seen, o = set(), []
        for s_ in seq:
            if s_ not in seen:
                seen.add(s_)
                o.append(s_)
        WP.order = o

    Bf = {}

    def alloc_chunk(nxn=1, nxs=1):
        Bf["xs"] = [sb([128, 8, TC], F32) for _ in range(nxs)]
        Bf["xq"] = sb([128, 8, TC], BF16)
        Bf["xn"] = [sb([128, 8, TC], BF16) for _ in range(nxn)]
        Bf["sd"] = sb([128, TC], F32)
        Bf["rstd"] = sb([128, TC], F32)

    def ln_dma(c):
        i = c % len(Bf["xs"])
        xs = Bf["xs"][i]
        cs = slice(c * TC, (c + 1) * TC)
        dma("sp", xs[:], xT3[:, :, cs], [], [("xs", i, k) for k in range(8)], f"xs{i}")

    def ln_sq(c):
        i = c % len(Bf["xs"])
        xs, xq = Bf["xs"][i], Bf["xq"]
        T("act", lambda e: e.activation(out=xq[:], in_=xs[:], func=AF.Square), [("xs", i, k) for k in range(8)], ["xq"])

    def ln_load(c):
        ln_dma(c)
        ln_sq(c)

    def ln_ss(c):
        xq, sd, rstd = Bf["xq"], Bf["sd"], Bf["rstd"]
        pb = psm()
        for kt in range(8):
            mm(ps[pb][:], ones[:], xq[:, kt, :], ["xq", "ones"], [("ps", pb)], start=(kt == 0), stop=(kt == 7))
        actf(sd[:], ps[pb][:], AF.Ln, [("ps", pb)], ["sd"], scale=1.0 / 1024, bias=EPS)
        actf(rstd[:], sd[:], AF.Exp, ["sd"], ["rstd"], scale=-0.5)

    def ln_xn(c, j):
        i = c % len(Bf["xs"])
        xs, xn, rstd = Bf["xs"][i], Bf["xn"], Bf["rstd"]
        for kt in range(8):
            stt("dve", xn[j][:, kt, :], xs[:, kt, :], gt[:, kt:kt + 1], rstd[:], ALU.mult, ALU.mult,
                [("xs", i, kt), "rstd", "gains"], [("xn", j, kt)])

    def load_norm(c, j):
        ln_load(c)
        ln_ss(c)
        ln_xn(c, j)

    def rms_fm(src, nt, nfeat, dstf, key_src, key_dst, g0):
        xq, sd, rstd = Bf["xq"], Bf["sd"], Bf["rstd"]
        T("act", lambda e: e.activation(out=xq[:, 0:nt, :], in_=src[:, 0:nt, :], func=AF.Square), [key_src(k) for k in range(nt)], ["xq"])
        pb = psm()
        for kt in range(nt):
            mm(ps[pb][:], ones[:], xq[:, kt, :], ["xq", "ones"], [("ps", pb)], start=(kt == 0), stop=(kt == nt - 1))
        actf(sd[:], ps[pb][:], AF.Ln, [("ps", pb)], ["sd"], scale=1.0 / nfeat, bias=EPS)
        actf(rstd[:], sd[:], AF.Exp, ["sd"], ["rstd"], scale=-0.5)
        for kt in range(nt):
            stt("dve", dstf(kt), src[:, kt, :], gt[:, g0 + kt:g0 + kt + 1], rstd[:], ALU.mult, ALU.mult, [key_src(kt), "rstd", "gains"], [key_dst(kt)])

    o_banks = [6, 7]

    def attention(tiles, lhsT_of, rhs_of, vx_of, exp_scale, kkeys, qkeys, vkeys, finish, side=()):
        flat = []
        for c in range(NCH):
            tl = tiles(c)
            for i, (kt, lo, masks) in enumerate(tl):
                flat.append((c, kt, lo, masks, i == 0, i == len(tl) - 1))
        pairs = [flat[i:i + 2] for i in range(0, len(flat), 2)]
        side = list(side)
        per = (len(side) + len(pairs) - 1) // max(1, len(pairs))
        slots = {}

        def qk(j):
            pd = slot()
            slots[j] = pd
            for u, (c, kt, lo, masks, first, last) in enumerate(pairs[j]):
                o = 512 * u
                mm(psd[pd][:, o + lo:o + TC], lhsT_of(kt), rhs_of(c, lo), kkeys + qkeys(c), [("psd", pd, u)], start=True, stop=(len(masks) == 0))
                for mi, (c0, n, mAP, mk) in enumerate(masks):
                    mm(psd[pd][:, o + c0:o + c0 + n], ident[:], mAP, ["ident"] + mk, [("psd", pd, u)], start=False, stop=(mi == len(masks) - 1))

        qk(0)
        if len(pairs) > 1:
            qk(1)
        for j in range(len(pairs)):
            if j + 2 < len(pairs):
                qk(j + 2)
            pd, pt = slots[j], j % 4
            pr = pairs[j]
            if len(pr) == 2 and pr[1][2] == 0:
                lo0 = pr[0][2]
                actf(Ptp[pt][:, lo0:2 * TC], psd[pd][:, lo0:2 * TC], AF.Exp, [("psd", pd, 0), ("psd", pd, 1)], [("Pt", pt, 0), ("Pt", pt, 1)], scale=exp_scale)
            else:
                for u, (c, kt, lo, masks, first, last) in enumerate(pr):
                    o = 512 * u
                    actf(Ptp[pt][:, o + lo:o + TC], psd[pd][:, o + lo:o + TC], AF.Exp, [("psd", pd, u)], [("Pt", pt, u)], scale=exp_scale)
            release(pd)
            for u, (c, kt, lo, masks, first, last) in enumerate(pr):
                o = 512 * u
                ob = o_banks[c % 2]
                mm(ps[ob][0:96, lo:TC], vx_of(kt), Ptp[pt][:, o + lo:o + TC], [("Pt", pt, u)] + vkeys(kt), [("ps", ob)], start=first, stop=last)
                if last:
                    finish(c, ob)
            for _ in range(per):
                if side:
                    side.pop(0)()
            itc[0] += 1
            run_deferred()
        while side:
            side.pop(0)()

    def make_finish(dstT, h):
        f, hh = h // 2, h % 2

        def finish(c, ob):
            cs = slice(c * TC, (c + 1) * TC)
            oi = ofn[0] % 3
            ofn[0] += 1
            of = ofs[oi]
            ok = ("ofs", oi)
            cp("dve", of[0:65, :], ps[ob][0:65, :], [("ps", ob)], [ok])
            rrf = rrfs[oi]
            rk = ("rrf", oi)
            T("dve", lambda e: e.reciprocal(out=rrf[64:65, :], in_=of[64:65, :]), [ok], [rk])

            def stage_b():
                pd = slot()
                mm(psd[pd][0:64, 0:TC], ones[64:65, 0:64], rrf[64:65, :], [rk, "ones"], [("psd", pd, 0)])
                if hh == 0:
                    tt("dve", dstT[0:64, f, cs], of[0:64, :], psd[pd][0:64, 0:TC], ALU.mult, [ok, ("psd", pd, 0)], [("oT", id(dstT), f, c, 0)])
                else:
                    sg = stg[c % 2]
                    tt("dve", sg[0:64, :], of[0:64, :], psd[pd][0:64, 0:TC], ALU.mult, [ok, ("psd", pd, 0)], [("stg", c % 2)])
                    dma("sp", dstT[64:128, f, cs], sg[0:64, :], [("stg", c % 2)], [("oT", id(dstT), f, c, 1)], f"stg{c % 2}")
                release(pd)
            DEF.append((itc[0] + 7, stage_b))
        return finish

    oT_keys = lambda dstT, c: [("oT", id(dstT), f, c, hh) for f in range(4) for hh in range(2)]

    def bc16(ap2, n):
        return ap2.unsqueeze(2).to_broadcast([128, 16, n])

    for g in range(2):
        top[0] = glob_top
        qT = sb([128, 2, SEQ], BF16)
        kT = sb([128, 2, SEQ], BF16)
        Vx = sb([128, 32, 4, 96], BF16)
        ksum = sb([128, 2, 16], F32)
        KM = sb([128, 2, 32], BF16)
        M16 = sb([128, 16, 16], F32)
        Mb = [sb([128, 256], BF16) for _ in range(2)]
        gs = sb([128, 16, 16], F32)
        g1 = sb([128, 16, 16], F32)
        eq = sb([128, 16, 16], F32)
        mx = sb([128, 16], F32)
        MT = sb([128, SEQ], BF16)
        a1_top = top[0]
        alloc_chunk(2, 2)
        xn = Bf["xn"]
        mset("pool", Vx[:, :, :, 64:96], 1.0, [("Vx", t_) for t_ in range(32)])
        mset("pool", KM[:], 0.0, ["KM"])
        if g == 0:
            WP.step(8)

        def gate_part1(c, qT=qT, KM=KM, M16=M16, Mb=Mb, gs=gs, g1=g1, eq=eq, mx=mx, ksum=ksum):
            for fl in range(2):
                ts("dve", KM[0:64, fl, 2 * c:2 * c + 2], ksum[0:64, fl, 2 * c:2 * c + 2], 1.0 / 256, ALU.mult, ["ksum"], ["KM"])
                ts("dve", KM[64:128, fl, 16 + 2 * c:18 + 2 * c], ksum[64:128, fl, 2 * c:2 * c + 2], 1.0 / 256, ALU.mult, ["ksum"], ["KM"])
            qlo, qhi = 2 * c, 2 * c + 1
            mset("pool", M16[:], NEG, ["M16"])
            if qhi <= 3:
                mset("pool", M16[:, 0:8, 0:qlo + 1], 0.0, ["M16"])
                mset("pool", M16[:, 8:16, 0:qhi + 1], 0.0, ["M16"])
            else:
                pb = psm()
                for t4 in range(4):
                    t = 4 * c + t4
                    for fl in range(2):
                        mm(ps[pb][:, t4 * 64 + fl * 32:t4 * 64 + (fl + 1) * 32], qT[:, fl, t * 128:(t + 1) * 128], KM[:, fl, :],
                           [("qT", fl, c), "KM"], [("ps", pb)])
                g3 = ps[pb][:, 0:256].rearrange("p (g n) -> p g n", g=16)[:, :, 0:qhi]
                cp("dve", gs[:, :, 0:qhi], g3, [("ps", pb)], ["gs"])
                T("dve", lambda e: e.memset(gs[:, 0:8, qlo:qlo + 1], -1.0e9), (), ["gs"])
                src = gs
                for rnd in range(3):
                    T("dve", lambda e, src=src: e.reduce_max(out=mx[:], in_=src[:, :, 0:qhi], axis=AX.X), ["gs", "g1"], ["mx"])
                    if rnd < 2:
                        tt("dve", eq[:, :, 0:qhi], src[:, :, 0:qhi], bc16(mx[:], qhi), ALU.is_ge, ["gs", "g1", "mx"], ["eq"])
                        stt("dve", g1[:, :, 0:qhi], eq[:, :, 0:qhi], -1.0e9, src[:, :, 0:qhi], ALU.mult, ALU.add, ["eq", "gs", "g1"], ["g1"])
                        src = g1
                tt("dve", eq[:, :, 0:qhi], gs[:, :, 0:qhi], bc16(mx[:], qhi), ALU.is_ge, ["gs", "mx"], ["eq"])
                ts("dve", M16[:, :, 0:qhi], eq[:, :, 0:qhi], -NEG, ALU.mult, ["eq"], ["M16"], s2=NEG, op1=ALU.add)
                T("dve", lambda e: e.memset(M16[:, 0:8, qlo:qlo + 1], 0.0), (), ["M16"])
                T("dve", lambda e: e.memset(M16[:, 8:16, qhi:qhi + 1], 0.0), (), ["M16"])
            cp("pool", Mb[c % 2][:], M16[:].rearrange("p g n -> p (g n)"), ["M16"], [("Mb", c % 2)])

        def gate_part2(c, Mb=Mb, MT=MT):
            for t4 in range(4):
                T("pe", lambda e, t4=t4, Mb=Mb: e.transpose(psT[0:64, t4 * 128:(t4 + 1) * 128], Mb[c % 2][:, t4 * 64:(t4 + 1) * 64], ident[:]),
                  [("Mb", c % 2), "ident"], [("ps", 7)])
            cp("act", MT[0:64, c * TC:(c + 1) * TC], psT[0:64, 0:TC], [("ps", 7)], [("MT", c)])

        ln_load(0)
        ln_ss(0)
        ln_xn(0, 0)
        ln_dma(1)
        for c in range(NCH):
            j = c % 2
            cs = slice(c * TC, (c + 1) * TC)
            if c + 2 < NCH:
                ln_dma(c + 2)
            if g == 0:
                WP.step(1)
            for fl in range(2):
                sl = W.get(SL_Q + 2 * g + fl)
                pb = psg()
                for kt in range(8):
                    mm(ps[pb][:], ring[sl][:, kt * 128:(kt + 1) * 128], xn[j][:, kt, :], [("ring", sl), ("xn", j, kt)], [("ps", pb)],
                       start=(kt == 0), stop=(kt == 7))
                actf(qT[:, fl, cs], ps[pb][:], AF.Identity, [("ps", pb)], [("qT", fl, c)], scale=0.125)
            if c + 1 < NCH:
                ln_sq(c + 1)
            if c > 0:
                gate_part2(c - 1)
            for fl in range(2):
                sl = W.get(SL_K + 2 * g + fl)
                pb = psg()
                for kt in range(8):
                    mm(ps[pb][:], ring[sl][:, kt * 128:(kt + 1) * 128], xn[j][:, kt, :], [("ring", sl), ("xn", j, kt)], [("ps", pb)],
                       start=(kt == 0), stop=(kt == 7))
                cp("act", kT[:, fl, cs], ps[pb][:], [("ps", pb)], [("kT", fl, c)])
                T("dve", lambda e, fl=fl, c=c, cs=cs, kT=kT, ksum=ksum: e.reduce_sum(
                    out=ksum[:, fl, 2 * c:2 * c + 2], in_=kT[:, fl, cs].rearrange("p (b t) -> p b t", b=2), axis=AX.X),
                  [("kT", fl, c)], ["ksum"])
            if c + 1 < NCH:
                ln_ss(c + 1)
            pbs = [psg() for _ in range(4)]
            for jv in range(4):
                sl = W.get(SL_V + jv)
                for t4 in range(4):
                    for kk in range(2):
                        kt = 2 * jv + kk
                        mm(ps[pbs[t4]][:, 0:256], xn[j][:, kt, t4 * 128:(t4 + 1) * 128],
                           ring[sl][:, kk * 512 + g * 256:kk * 512 + (g + 1) * 256],
                           [("ring", sl), ("xn", j, kt)], [("ps", pbs[t4])], start=(kt == 0), stop=(kt == 7))
            if c + 1 < NCH:
                ln_xn(c + 1, (c + 1) % 2)
            for t4 in range(4):
                cp("act", Vx[:, 4 * c + t4, :, 0:64], ps[pbs[t4]][:, 0:256].rearrange("p (h d) -> p h d", h=4),
                   [("ps", pbs[t4])], [("Vx", 4 * c + t4)])
            gate_part1(c)
        gate_part2(NCH - 1)
        S.barrier()

        top[0] = a1_top
        Qaug = [sb([128, SEQ], BF16) for _ in range(2)]
        Kaug = [sb([128, SEQ], BF16) for _ in range(2)]
        Dtmp = sb([128, 8 * 256], F32)
        caus = sb([128, 256], F32)
        b31 = sb([128, 8], F32)
        Dm = sb([128, 8, 256], BF16)
        dma("sp", Dtmp[:], c_Tm, [], ["Dtmp"], "c3")
        dma("sp", caus[:], c_caus, [], ["caus"], "c4")
        dma("sp", b31[:], c_b31, [], ["b31"], "c5")
        for h in range(8):
            stt("dve", Dm[:, h, :], Dtmp[:, h * 256:(h + 1) * 256], b31[:, h:h + 1], caus[:], ALU.subtract, ALU.add,
                ["Dtmp", "b31", "caus"], [("Dm", h)])
        for b in range(2):
            mset("pool", Qaug[b][64:128, :], 0.0, [("Qaug", b, "m")])
            mset("pool", Kaug[b][64:128, :], 0.0, [("Kaug", b, "e")])
            dma("pool", Kaug[b][64:80, :], c_E, [], [("Kaug", b, "e")], f"ce{b}")

        def moba_tiles(h):
            def tiles(c):
                out = []
                for kt in range(4 * c + 4):
                    jj = kt - 4 * c
                    lo = max(0, 128 * jj)
                    masks = []
                    if jj == -1:
                        masks = [(0, 128, Dm[:, h, 128:256], [("Dm", h)])]
                    elif jj >= 0:
                        n = min(256, TC - 128 * jj)
                        masks = [(128 * jj, n, Dm[:, h, 0:n], [("Dm", h)])]
                    out.append((kt, lo, masks))
                return out
            return tiles

        def head_copies(hl):
            b, fl, hh = hl % 2, hl // 2, hl % 2
            dma("sp", Qaug[b][0:64, :], qT[64 * hh:64 * hh + 64, fl, :], [("qT", fl, c) for c in range(NCH)], [("Qaug", b, "q")], f"qa{b}")
            dma("sp", Kaug[b][0:64, :], kT[64 * hh:64 * hh + 64, fl, :], [("kT", fl, c) for c in range(NCH)], [("Kaug", b, "k")], f"ka{b}")
            dma("sp", Qaug[b][64:80, :], MT[16 * hl:16 * hl + 16, :], [("MT", i) for i in range(8)], [("Qaug", b, "m")], f"qm{b}")

        head_copies(0)
        for hl in range(4):
            h = 4 * g + hl
            b = hl % 2
            side = []
            if hl + 1 < 4:
                side.append(lambda hl=hl: head_copies(hl + 1))
            side += [(lambda: WP.step(1)) for _ in range(40)]
            attention(moba_tiles(h),
                      lambda kt, b=b, Kaug=Kaug: Kaug[b][:, kt * 128:(kt + 1) * 128],
                      lambda c, lo, b=b, Qaug=Qaug: Qaug[b][:, c * TC + lo:(c + 1) * TC],
                      lambda kt, hl=hl, Vx=Vx: Vx[:, kt, hl, :],
                      1.0,
                      [("Kaug", b, "k"), ("Kaug", b, "e")],
                      lambda c, b=b: [("Qaug", b, "q"), ("Qaug", b, "m")],
                      lambda kt: [("Vx", kt)],
                      make_finish(oaT, h), side)
        run_deferred(True)
        S.barrier()

    top[0] = glob_top
    obT = sb([128, 4, SEQ], BF16)
    cqnT = sb([128, 3, SEQ], BF16)
    ckvnT = sb([128, 2, SEQ], BF16)
    krT = sb([128, SEQ], BF16)
    mla_top = top[0]
    alloc_chunk()
    xn = Bf["xn"]
    cqf = sb([128, 3, TC], F32)
    ckvf = sb([128, 2, TC], F32)
    cst = [sb([128, TC], F32) for _ in range(1)] * 2
    snt = [sb([128, TC], F32) for _ in range(1)] * 2
    tabmod = [1]
    t1 = sb([128, TC], F32)
    t2 = sb([128, TC], F32)
    a2_top = top[0]
    top[0] = mla_top
    cst2 = [sb([128, TC], F32) for _ in range(2)]
    snt2 = [sb([128, TC], F32) for _ in range(2)]
    t1b = sb([128, TC], F32)
    t2b = sb([128, TC], F32)
    KT = [sb([128, SEQ], BF16) for _ in range(2)]
    QTh = [sb([128, SEQ], BF16) for _ in range(2)]
    Vh = [sb([128, 32, 96], BF16) for _ in range(2)]

    tabn = [0]

    def load_tables(c):
        i = tabn[0] % tabmod[0]
        tabn[0] += 1
        cs = slice(c * TC, (c + 1) * TC)
        dma("sp", cst[i][64:96, :], c_cos[:, cs], [], [("cst", i)], f"cst{i}")
        dma("sp", snt[i][64:96, :], c_sin[:, cs], [], [("snt", i)], f"snt{i}")
        return i

    def rope(dst, p1, p2, ti, rkeys, wkeys):
        tt("dve", t1[64:96, :], p1, cst[ti][64:96, :], ALU.mult, rkeys[0:1] + [("cst", ti)], ["t1"])
        tt("dve", t2[64:96, :], p2, snt[ti][64:96, :], ALU.mult, rkeys[1:2] + [("snt", ti)], ["t2"])
        tt("pool", dst, t1[64:96, :], t2[64:96, :], ALU.add, ["t1", "t2"], wkeys)

    ln_load(0)
    ln_ss(0)
    ln_xn(0, 0)
    for c in range(NCH):
        j = 0
        cs = slice(c * TC, (c + 1) * TC)
        if c + 1 < NCH:
            ln_load(c + 1)
        ti = load_tables(c)
        for f in range(3):
            sl = W.get(SL_CQ + f)
            pb = psg()
            for kt in range(8):
                mm(ps[pb][:], ring[sl][:, kt * 128:(kt + 1) * 128], xn[j][:, kt, :], [("ring", sl), ("xn", j, kt)], [("ps", pb)],
                   start=(kt == 0), stop=(kt == 7))
            cp("act", cqf[:, f, :], ps[pb][:], [("ps", pb)], ["cqf"])
        for f in range(2):
            sl = W.get(SL_CKV + f)
            pb = psg()
            for kt in range(8):
                mm(ps[pb][:], ring[sl][:, kt * 128:(kt + 1) * 128], xn[j][:, kt, :], [("ring", sl), ("xn", j, kt)], [("ps", pb)],
                   start=(kt == 0), stop=(kt == 7))
            cp("dve", ckvf[:, f, :], ps[pb][:], [("ps", pb)], ["ckvf"])
        pk = []
        for i in range(2):
            sl = W.get(SL_KR + i)
            pb = psg()
            pk.append(pb)
            for kt in range(8):
                mm(ps[pb][0:96, :], ring[sl][:, kt * 128:kt * 128 + 96], xn[j][:, kt, :], [("ring", sl), ("xn", j, kt)], [("ps", pb)],
                   start=(kt == 0), stop=(kt == 7))
        rope(krT[64:96, cs], ps[pk[0]][64:96, :], ps[pk[1]][64:96, :], ti, [("ps", pk[0]), ("ps", pk[1])], [("krT", c)])
        rms_fm(cqf, 3, 384.0, lambda kt, cs=cs: cqnT[:, kt, cs], lambda k: "cqf", lambda kt, c=c: ("cqnT", c), 16)
        if c + 1 < NCH:
            ln_ss(c + 1)
            ln_xn(c + 1, 0)
        rms_fm(ckvf, 2, 256.0, lambda kt, cs=cs: ckvnT[:, kt, cs], lambda k: "ckvf", lambda kt, c=c: ("ckvnT", c), 19)
    S.barrier()
    cst[0], cst[1], snt[0], snt[1] = cst2[0], cst2[1], snt2[0], snt2[1]
    tabmod[0] = 2
    t1, t2 = t1b, t2b

    WP.step(NSLAB)
    for b in range(2):
        mset("pool", KT[b][96:128, :], 0.0, [("KT", b, "z")])
        mset("pool", QTh[b][96:128, :], 0.0, [("QT", b, "z")])
        mset("pool", Vh[b][:, :, 64:96], 1.0, [("Vh", b, g_) for g_ in range(4)])

    def mla_tiles(c):
        out = []
        for kt in range(4 * c + 4):
            jj = kt - 4 * c
            lo = max(0, 128 * jj)
            masks = [(128 * jj, 128, tri[:], ["tri"])] if jj >= 0 else []
            out.append((kt, lo, masks))
        return out

    def rope_dve(dst, p1, p2, ti, rkeys, wkeys):
        tt("dve", t1[64:96, :], p1, cst[ti][64:96, :], ALU.mult, rkeys[0:1] + [("cst", ti)], ["t1"])
        tt("dve", t2[64:96, :], p2, snt[ti][64:96, :], ALU.mult, rkeys[1:2] + [("snt", ti)], ["t2"])
        tt("dve", dst, t1[64:96, :], t2[64:96, :], ALU.add, ["t1", "t2"], wkeys)

    def mla_build(h):
        b = h % 2
        st = {}
        out = []

        def get_slabs():
            st["sl"] = W.get(SL_HEAD + h)
            st["slv"] = W.get(SL_UV)
        out.append(get_slabs)

        def kbuild(c):
            sl = st["sl"]
            cs = slice(c * TC, (c + 1) * TC)
            pd = slot()
            for kt in range(2):
                mm(psd[pd][0:64, 0:TC], ring[sl][:, (6 + kt) * 128:(6 + kt) * 128 + 64], ckvnT[:, kt, cs], [("ring", sl), ("ckvnT", c)], [("psd", pd, 0)],
                   start=(kt == 0), stop=(kt == 1))
            cp("act", KT[b][0:64, cs], psd[pd][0:64, 0:TC], [("psd", pd, 0)], [("KT", b, "n", c)])
            release(pd)
            cp("pool", KT[b][64:96, cs], krT[64:96, cs], [("krT", c)], [("KT", b, "r", c)])

        def vbuild(g_):
            slv = st["slv"]
            pd = slot()
            for i in range(8):
                t = 8 * g_ + i
                for kt in range(2):
                    mm(psd[pd][:, i * 64:(i + 1) * 64], ckvnT[:, kt, t * 128:(t + 1) * 128], ring[slv][:, kt * 512 + h * 64:kt * 512 + h * 64 + 64],
                       [("ring", slv), ("ckvnT", t // 4)], [("psd", pd, 0)], start=(kt == 0), stop=(kt == 1))
            cp("act", Vh[b][:, 8 * g_:8 * g_ + 8, 0:64], psd[pd][:, 0:TC].rearrange("p (i d) -> p i d", i=8), [("psd", pd, 0)], [("Vh", b, g_)])
            release(pd)

        def qbuild(c):
            sl = st["sl"]
            cs = slice(c * TC, (c + 1) * TC)
            ti = load_tables(c)
            pd = slot()
            for kt in range(3):
                mm(psd[pd][0:96, 0:TC], ring[sl][:, kt * 128:kt * 128 + 96], cqnT[:, kt, cs], [("ring", sl), ("cqnT", c)], [("psd", pd, 0)],
                   start=(kt == 0), stop=(kt == 2))
            for kt in range(3):
                mm(psd[pd][0:96, TC:2 * TC], ring[sl][:, (3 + kt) * 128:(3 + kt) * 128 + 96], cqnT[:, kt, cs], [("ring", sl), ("cqnT", c)], [("psd", pd, 1)],
                   start=(kt == 0), stop=(kt == 2))
            cp("act", QTh[b][0:64, cs], psd[pd][0:64, 0:TC], [("psd", pd, 0)], [("QT", b, "n", c)])
            rope_dve(QTh[b][64:96, cs], psd[pd][64:96, 0:TC], psd[pd][64:96, TC:2 * TC], ti, [("psd", pd, 0), ("psd", pd, 1)], [("QT", b, "r", c)])
            release(pd)

        for c in range(NCH):
            out.append(lambda c=c: kbuild(c))
        for g_ in range(4):
            out.append(lambda g_=g_: vbuild(g_))
        for c in range(NCH):
            out.append(lambda c=c: qbuild(c))
        return out

    for fn in mla_build(0):
        fn()
    for h in range(8):
        b = h % 2
        side = []
        if h + 1 < 8:
            for fn in mla_build(h + 1):
                fn()
        attention(mla_tiles,
                  lambda kt, b=b: KT[b][:, kt * 128:(kt + 1) * 128],
                  lambda c, lo, b=b: QTh[b][:, c * TC + lo:(c + 1) * TC],
                  lambda kt, b=b: Vh[b][:, kt, :],
                  96.0 ** -0.5,
                  [("KT", b, "z")] + [("KT", b, "n", c) for c in range(NCH)] + [("KT", b, "r", c) for c in range(NCH)],
                  lambda c, b=b: [("QT", b, "n", c), ("QT", b, "r", c), ("QT", b, "z")],
                  lambda kt, b=b: [("Vh", b, kt // 8)],
                  make_finish(obT, h), side)
    run_deferred(True)
    S.barrier()
    if DEBUG:
        dma("sp", dbg_oa, oaT[:].rearrange("p a t -> p (a t)"), [], [], "dbg")
        dma("sp", dbg_ob, obT[:].rearrange("p a t -> p (a t)"), [], [], "dbg")
        S.barrier()
    top[0] = glob_top
    obT_c = sb([128, 4, SEQ], BF16)
    xs2 = [sb([128, 8, TC], F32) for _ in range(2)]
    xq = sb([128, 8, TC], BF16)
    xnc = sb([128, 8, TC], BF16)
    sd = sb([128, TC], F32)
    rstd = sb([128, TC], F32)
    r2 = sb([128, TC], F32)
    mg = sb([128, 8, TC], BF16)
    mnb = sb([128, 8, TC], BF16)
    upT = sb([128, 32, TC], BF16)
    _cur = top[0]
    top[0] = att_base
    sga = sb([128, TC], F32)
    sgb = sb([128, TC], F32)
    ta = sb([128, TC], F32)
    tb_ = sb([128, TC], F32)
    rl = [sb([128, TC], F32) for _ in range(2)]
    assert top[0] <= glob_top
    top[0] = _cur

    def xk(i):
        return [("xs", i, k) for k in range(8)]

    def c_load(c):
        i = c % 2
        dma("sp", xs2[i][:], xT3[:, :, c * TC:(c + 1) * TC], [], xk(i), f"xs{i}")

    def c_norm_a(c):
        i = c % 2
        T("act", lambda e: e.activation(out=xq[:], in_=xs2[i][:], func=AF.Square), xk(i), ["xq"])

    def c_norm_b(c):
        i = c % 2
        for kt in range(8):
            mm(ps[6][:], ones[:], xq[:, kt, :], ["xq", "ones"], [("ps", 6)], start=(kt == 0), stop=(kt == 7))
        actf(sd[:], ps[6][:], AF.Ln, [("ps", 6)], ["sd"], scale=1.0 / 1024, bias=EPS)
        actf(rstd[:], sd[:], AF.Exp, ["sd"], ["rstd"], scale=-0.5)
        for kt in range(8):
            stt("dve", xnc[:, kt, :], xs2[i][:, kt, :], gt[:, kt:kt + 1], rstd[:], ALU.mult, ALU.mult,
                [("xs", i, kt), "rstd", "gains"], [("xn", kt)])

    c_load(0)
    c_norm_a(0)
    c_norm_b(0)
    for c in range(NCH):
        cs = slice(c * TC, (c + 1) * TC)
        i = c % 2
        xs = xs2[i]
        if c + 1 < NCH:
            c_load(c + 1)
        for f in range(8):
            sa, sbb, sp_ = W.get(SL_G + f), W.get(SL_G + 8 + f), W.get(SL_P + f)
            pga, pgb, ppa, ppb = psg(), psg(), psg(), psg()
            for kt in range(8):
                mm(ps[pga][:], ring[sa][:, kt * 128:(kt + 1) * 128], xnc[:, kt, :], [("ring", sa), ("xn", kt)], [("ps", pga)],
                   start=(kt == 0), stop=(kt == 7))
            for kt in range(8):
                mm(ps[pgb][:], ring[sbb][:, kt * 128:(kt + 1) * 128], xnc[:, kt, :], [("ring", sbb), ("xn", kt)], [("ps", pgb)],
                   start=(kt == 0), stop=(kt == 7))
            for kt in range(4):
                mm(ps[ppa][:], ring[sp_][:, kt * 128:(kt + 1) * 128], oaT[:, kt, cs], [("ring", sp_)] + oT_keys(oaT, c), [("ps", ppa)],
                   start=(kt == 0), stop=(kt == 3))
            for kt in range(4):
                mm(ps[ppb][:], ring[sp_][:, (4 + kt) * 128:(5 + kt) * 128], obT[:, kt, cs], [("ring", sp_)] + oT_keys(obT, c), [("ps", ppb)],
                   start=(kt == 0), stop=(kt == 3))
            actf(sga[:], ps[pga][:], AF.Sigmoid, [("ps", pga)], ["sga"])
            actf(sgb[:], ps[pgb][:], AF.Sigmoid, [("ps", pgb)], ["sgb"])
            tt("dve", ta[:], sga[:], ps[ppa][:], ALU.mult, ["sga", ("ps", ppa)], ["ta"])
            tt("dve", tb_[:], sgb[:], ps[ppb][:], ALU.mult, ["sgb", ("ps", ppb)], ["tb"])
            tt("pool", mg[:, f, :], ta[:], tb_[:], ALU.add, ["ta", "tb"], [("mg", f)])
        for f in range(8):
            sl = W.get(SL_O + f)
            pb = psg()
            for kt in range(8):
                mm(ps[pb][:], ring[sl][:, kt * 128:(kt + 1) * 128], mg[:, kt, :], [("ring", sl), ("mg", kt)], [("ps", pb)],
                   start=(kt == 0), stop=(kt == 7))
            tt("dve", xs[:, f, :], xs[:, f, :], ps[pb][:], ALU.add, [("xs", i, f), ("ps", pb)], [("xs", i, f)])
            ts("dve", mnb[:, f, :], xs[:, f, :], gt[:, 8 + f:9 + f], ALU.mult, [("xs", i, f), "gains"], [("mnb", f)])
        if DEBUG:
            dma("sp", dbg_h[:, :, cs], xs[:], xk(i), [], "dbg")
            dma("sp", dbg_mg[:, :, cs], mg[:], [("mg", k) for k in range(8)], [], "dbg")
        T("act", lambda e, xs=xs: e.activation(out=xq[:], in_=xs[:], func=AF.Square), xk(i), ["xq"])
        for kt in range(8):
            mm(ps[6][:], ones[:], xq[:, kt, :], ["xq", "ones"], [("ps", 6)], start=(kt == 0), stop=(kt == 7))
        actf(sd[:], ps[6][:], AF.Ln, [("ps", 6)], ["sd"], scale=1.0 / 1024, bias=EPS)
        actf(r2[:], sd[:], AF.Exp, ["sd"], ["r2"], scale=-1.0)
        for ff in range(32):
            sl = W.get(SL_UP + ff)
            pb = psg()
            for kt in range(8):
                mm(ps[pb][:], ring[sl][:, kt * 128:(kt + 1) * 128], mnb[:, kt, :], [("ring", sl), ("mnb", kt)], [("ps", pb)],
                   start=(kt == 0), stop=(kt == 7))
            actf(rl[ff % 2][:], ps[pb][:], AF.Relu, [("ps", pb)], [("rl", ff % 2)])
            tt("pool", upT[:, ff, :], rl[ff % 2][:], rl[ff % 2][:], ALU.mult, [("rl", ff % 2)], [("upT", ff)])
            if c + 1 < NCH:
                if ff == 4:
                    c_norm_a(c + 1)
                if ff == 12:
                    c_norm_b(c + 1)
        for f in range(8):
            pb = psg()
            for jd in range(4):
                sl = W.get(SL_DN + f * 4 + jd)
                for i8 in range(8):
                    ff = 8 * jd + i8
                    mm(ps[pb][:], ring[sl][:, i8 * 128:(i8 + 1) * 128], upT[:, ff, :], [("ring", sl), ("upT", ff)], [("ps", pb)],
                       start=(ff == 0), stop=(ff == 31))
            tt("dve", ta[:], ps[pb][:], r2[:], ALU.mult, [("ps", pb), "r2"], ["ta"])
            tt("pool", xs[:, f, :], xs[:, f, :], ta[:], ALU.add, [("xs", i, f), "ta"], [("xs", i, f)])
        if DEBUG:
            dma("sp", dbg_h2[:, :, cs], xs[:], xk(i), [], "dbg")
        T("act", lambda e, xs=xs: e.activation(out=xq[:], in_=xs[:], func=AF.Square), xk(i), ["xq"])
        for kt in range(8):
            mm(ps[7][:], ones[:], xq[:, kt, :], ["xq", "ones"], [("ps", 7)], start=(kt == 0), stop=(kt == 7))
        actf(sd[:], ps[7][:], AF.Ln, [("ps", 7)], ["sd"], scale=1.0 / 1024, bias=EPS)
        actf(rstd[:], sd[:], AF.Exp, ["sd"], ["rstd"], scale=-0.5)
        for f in range(8):
            stt("dve", xs[:, f, :], xs[:, f, :], gt[:, 21 + f:22 + f], rstd[:], ALU.mult, ALU.mult,
                [("xs", i, f), "gfin", "rstd"], [("xs", i, f)])
        dma("pool", outT3[:, :, cs], xs[:], xk(i), [("out", c)], f"out{i}")

    if seq is None:
        return rec

    S.assign()
    sems = {s_: nc.alloc_semaphore("s_" + "_".join(map(str, s_))) for s_ in S.semnames()}
    final_out = [(("chan", f"out{i}"), S.chan_count[f"out{i}"]) for i in range(2)]

    def mk(ename):
        def f(h):
            S.emit_engine(ename, h, sems)
            if ename == "pool":
                for sname, v in final_out:
                    h.wait_ge(sems[sname], v)
        return f

    with nc.allow_low_precision(reason="bf16 matmul operands by design; fp32 accumulation"), nc.Block() as block:
        block.tensor(mk("pe"))
        block.scalar(mk("act"))
        block.vector(mk("dve"))
        block.gpsimd(mk("pool"))
        block.sync(mk("sp"))
    return nc


def _bucket(d):
    d = np.maximum(d, 0)
    dd = np.maximum(d, 1).astype(np.float32)
    large = 16 + (np.log(dd / np.float32(16)) / np.float32(math.log(128 / 16)) * np.float32(16)).astype(np.int32)
    large = np.minimum(large, 31)
    return np.where(d < 16, d, large)


def _host_layout(w_in, rel_bias, mla_q_norm, w_uq, mla_kv_norm, w_uk, w_uv, w_proj_a, w_proj_b, w_out,
                 norm_attn, norm_mlp, w_mlp_up, w_mlp_down, norm_final):
    f32 = np.float32
    w_in, w_uq, w_uk, w_uv = w_in[0], w_uq[0], w_uk[0], w_uv[0]
    wpa, wpb, wo, wup, wdn = w_proj_a[0], w_proj_b[0], w_out[0], w_mlp_up[0], w_mlp_down[0]
    ga, gm, gq, gkv = norm_attn[0], norm_mlp[0], mla_q_norm[0], mla_kv_norm[0]
    wsrc = np.zeros((NSLAB, 128, 1024), f32)

    def fm(s, Wm, f, nk, g=None, slot0=0, k0=0):
        for kt in range(nk):
            wsrc[s, :, (slot0 + kt) * 128:(slot0 + kt + 1) * 128] = Wm[(k0 + kt) * 128:(k0 + kt + 1) * 128, f * 128:(f + 1) * 128]

    for f in range(4):
        fm(SL_Q + f, w_in[:, 0:512], f, 8, ga)
        fm(SL_K + f, w_in[:, 512:1024], f, 8, ga)
    for j in range(4):
        for kk in range(2):
            kt = 2 * j + kk
            wsrc[SL_V + j, :, kk * 512:(kk + 1) * 512] = w_in[kt * 128:(kt + 1) * 128, 1024:1536]
    for f in range(3):
        fm(SL_CQ + f, w_in[:, 1536:1920], f, 8, ga)
    for f in range(2):
        fm(SL_CKV + f, w_in[:, 1920:2176], f, 8, ga)
    kr = w_in[:, 2176:2208]
    krs = np.concatenate([kr[:, 16:32], kr[:, 0:16]], axis=1)
    for kt in range(8):
        wsrc[SL_KR, :, kt * 128 + 64:kt * 128 + 96] = kr[kt * 128:(kt + 1) * 128]
        wsrc[SL_KR + 1, :, kt * 128 + 64:kt * 128 + 96] = krs[kt * 128:(kt + 1) * 128]
    for kt in range(2):
        wsrc[SL_UV, :, kt * 512:(kt + 1) * 512] = w_uv[kt * 128:(kt + 1) * 128]
    for h in range(8):
        s = SL_HEAD + h
        qh = w_uq[:, h * 96:(h + 1) * 96]
        qsw = np.concatenate([qh[:, 80:96], qh[:, 64:80]], axis=1)
        for kt in range(3):
            wsrc[s, :, kt * 128:kt * 128 + 96] = qh[kt * 128:(kt + 1) * 128]
            wsrc[s, :, (3 + kt) * 128 + 64:(3 + kt) * 128 + 96] = qsw[kt * 128:(kt + 1) * 128]
        for kt in range(2):
            wsrc[s, :, (6 + kt) * 128:(6 + kt) * 128 + 64] = w_uk[kt * 128:(kt + 1) * 128, h * 64:(h + 1) * 64]
    for f in range(16):
        fm(SL_G + f, w_in[:, 2208:4256], f, 8, ga)
    for f in range(8):
        fm(SL_P + f, wpa, f, 4)
        fm(SL_P + f, wpb, f, 4, slot0=4)
        fm(SL_O + f, wo, f, 8)
        for jd in range(4):
            fm(SL_DN + f * 4 + jd, wdn, f, 8, k0=8 * jd)
    for ff in range(32):
        fm(SL_UP + ff, wup, ff, 8, gm)
    c_gains = np.zeros((128, 32), f32)
    c_gains[:, 0:8] = ga.reshape(8, 128).T
    c_gains[:, 8:16] = gm.reshape(8, 128).T
    c_gains[:, 16:19] = gq.reshape(3, 128).T
    c_gains[:, 19:21] = gkv.reshape(2, 128).T
    c_gains[:, 21:29] = norm_final.reshape(8, 128).T

    p = np.arange(128)[:, None]
    c_tri = np.where(p > np.arange(128)[None, :], NEG, 0.0).astype(f32)
    c_E = (np.arange(SEQ)[None, :] // 256 == np.arange(16)[:, None]).astype(f32)
    jj = np.arange(256)[None, :]
    c_caus = np.where(jj < p, NEG, 0.0).astype(f32)
    bk = _bucket(np.maximum(jj - p, 0))
    c_Tm = np.ascontiguousarray(np.transpose(rel_bias[bk], (0, 2, 1)).reshape(128, 8 * 256)).astype(f32)
    c_b31 = np.ascontiguousarray(np.broadcast_to(rel_bias[31][None, :], (128, 8))).astype(f32)
    half = 16
    inv = (10000.0 ** (-np.arange(half, dtype=np.float32) / half)).astype(np.float32)
    ang = np.arange(SEQ, dtype=np.float32)[None, :] * inv[:, None]
    cosv, sinv = np.cos(ang).astype(f32), np.sin(ang).astype(f32)
    c_cos = np.concatenate([cosv, cosv], 0)
    c_sin = np.concatenate([-sinv, sinv], 0)
    return dict(wsrc=wsrc, c_gains=c_gains, c_tri=c_tri, c_E=c_E, c_caus=c_caus, c_Tm=c_Tm, c_b31=c_b31,
                c_cos=c_cos, c_sin=c_sin)


_CACHE = {}


def kernel(x, w_in, rel_bias, mla_q_norm, w_uq, mla_kv_norm, w_uk, w_uv, w_proj_a, w_proj_b,
           w_out, norm_attn, norm_mlp, w_mlp_up, w_mlp_down, norm_final):
    x = np.asarray(x, np.float32)
    args = [np.asarray(a, np.float32) for a in (w_in, rel_bias, mla_q_norm, w_uq, mla_kv_norm, w_uk, w_uv, w_proj_a,
                                                w_proj_b, w_out, norm_attn, norm_mlp, w_mlp_up, w_mlp_down, norm_final)]
    shared = _host_layout(*args)
    if "nc" not in _CACHE:
        seq = build(None)
        _CACHE["nc"] = build(seq)
    nc = _CACHE["nc"]
    in_maps = []
    for b in range(8):
        m = dict(shared)
        m["xT"] = np.ascontiguousarray(x[b].T)
        in_maps.append(m)
    res = run_bass_kernel_spmd(nc, in_maps, core_ids=list(range(8)))
    out = np.stack([np.ascontiguousarray(r["outT"].T) for r in res.results], axis=0)
    return out.astype(np.float32)
```

```python
import math
import numpy as np
import concourse.bass as bass
import concourse.mybir as mybir
from concourse.bass_utils import run_bass_kernel_spmd

F32 = mybir.dt.float32
BF16 = mybir.dt.bfloat16
ALU = mybir.AluOpType
AF = mybir.ActivationFunctionType
AX = mybir.AxisListType

SEQ = 4096
TC = 512
NCH = SEQ // TC
NEG = -30000.0
EPS = 1e-6
RING = 8
SL_Q, SL_K, SL_V, SL_CQ, SL_CKV, SL_KR, SL_UV, SL_HEAD = 0, 4, 8, 12, 15, 17, 19, 20
SL_G, SL_P, SL_O, SL_UP, SL_DN = 28, 44, 52, 60, 92
NSLAB = 124


class Task:
    __slots__ = ("eng", "fn", "deps", "sem", "val", "ndma", "has_waiter")

    def __init__(self, eng, fn, ndma):
        self.eng, self.fn, self.ndma = eng, fn, ndma
        self.deps, self.sem, self.val, self.has_waiter = [], None, None, False


class Sched:
    ENGS = ("pe", "act", "dve", "pool", "sp")

    def __init__(self):
        self.prog = {e: [] for e in self.ENGS}
        self.last_w, self.readers, self.chan_count, self.chan_last = {}, {}, {}, {}

    def add(self, eng, fn, reads=(), writes=(), chan=None, ndma=0, extra=()):
        t = Task(eng, fn, ndma)
        if chan is not None:
            c = self.chan_count.get(chan, 0) + 16 * ndma
            self.chan_count[chan] = c
            t.sem, t.val = ("chan", chan), c
            self.chan_last[chan] = t
        deps = {}
        for k in reads:
            w = self.last_w.get(k)
            if w is not None:
                deps[id(w)] = (w, "raw")
        for k in writes:
            w = self.last_w.get(k)
            if w is not None and id(w) not in deps:
                deps[id(w)] = (w, "waw")
            for r in self.readers.get(k, ()):
                if id(r) not in deps:
                    deps[id(r)] = (r, "war")
        for w in extra:
            deps[id(w)] = (w, "raw")
        for w, kind in deps.values():
            if w is t:
                continue
            if w.ndma == 0 and ndma == 0 and w.eng == eng and eng == "pe":
                continue
            t.deps.append(w)
            w.has_waiter = True
        for k in reads:
            self.readers.setdefault(k, []).append(t)
        for k in writes:
            self.last_w[k] = t
            self.readers[k] = []
        self.prog[eng].append(t)
        return t

    def barrier(self):
        first = [self.prog[e][-1] for e in self.ENGS if self.prog[e]]
        ex = first + list(self.chan_last.values())
        for e in self.ENGS:
            self.add(e, lambda h: h.nop(), extra=ex)

    def assign(self):
        for e in self.ENGS:
            cnt = 0
            for t in self.prog[e]:
                if t.ndma == 0 and t.has_waiter:
                    cnt += 1
                    t.sem, t.val = ("eng", e), cnt

    def semnames(self):
        s = set(("eng", e) for e in self.ENGS)
        for e in self.ENGS:
            for t in self.prog[e]:
                if t.sem is not None:
                    s.add(t.sem)
        return sorted(s)

    def emit_engine(self, e, h, sems):
        known = {}
        for t in self.prog[e]:
            need = {}
            for d in t.deps:
                if d.val > need.get(d.sem, 0):
                    need[d.sem] = d.val
            for s, v in need.items():
                if known.get(s, 0) >= v:
                    continue
                h.wait_ge(sems[s], v)
                known[s] = v
            r = t.fn(h)
            if t.ndma > 0:
                for ins in r:
                    ins.then_inc(sems[t.sem], 16)
            elif t.has_waiter:
                r.then_inc(sems[t.sem], 1)


DEBUG = False


def build(seq):
    nc = bass.Bass("TRN2", target_bir_lowering=False)
    din = lambda n, s: nc.dram_tensor(n, s, F32, kind="ExternalInput").ap()
    xT = din("xT", [1024, SEQ])
    wsrc = din("wsrc", [NSLAB, 128, 1024])
    c_gains = din("c_gains", [128, 32])
    c_tri = din("c_tri", [128, 128])
    c_E = din("c_E", [16, SEQ])
    c_caus = din("c_caus", [128, 256])
    c_Tm = din("c_Tm", [128, 8 * 256])
    c_b31 = din("c_b31", [128, 8])
    c_cos = din("c_cos", [32, SEQ])
    c_sin = din("c_sin", [32, SEQ])
    outT = nc.dram_tensor("outT", [1024, SEQ], F32, kind="ExternalOutput").ap()
    wscr = nc.dram_tensor("wscr", [NSLAB, 128, 1024], BF16, kind="Internal").ap()
    if DEBUG:
        dbg_oa = nc.dram_tensor("dbg_oa", [128, 4 * SEQ], BF16, kind="ExternalOutput").ap()
        dbg_ob = nc.dram_tensor("dbg_ob", [128, 4 * SEQ], BF16, kind="ExternalOutput").ap()
        dbg_h = nc.dram_tensor("dbg_h", [128, 8, SEQ], F32, kind="ExternalOutput").ap()
        dbg_h2 = nc.dram_tensor("dbg_h2", [128, 8, SEQ], F32, kind="ExternalOutput").ap()
        dbg_mg = nc.dram_tensor("dbg_mg", [128, 8, SEQ], BF16, kind="ExternalOutput").ap()
    xT3 = xT.rearrange("(k p) t -> p k t", p=128)
    outT3 = outT.rearrange("(k p) t -> p k t", p=128)

    S = Sched()
    rec = [] if seq is None else None

    SB_BASE = 16512 + 64
    top = [SB_BASE]
    cnt = [0]

    def sb(shape, dt):
        n = 1
        for s_ in shape[1:]:
            n *= s_
        nbytes = n * (4 if dt == F32 else 2)
        nbytes = (nbytes + 63) // 64 * 64
        off = top[0]
        top[0] += nbytes
        assert top[0] <= 229376, ("SBUF overflow", top[0])
        cnt[0] += 1
        return nc.alloc_sbuf_tensor_at(f"t{cnt[0]}", shape, dt, offset=off)

    psd = [nc.alloc_psum_tensor(f"psd{i}", [128, 1024], F32) for i in range(4)]
    ps = [psd[b // 2][:, (b % 2) * 512:(b % 2 + 1) * 512] for b in range(8)]
    psT = psd[3].bitcast(BF16)[:, 1024:2048]
    rr_b = [0]

    def psb():
        rr_b[0] ^= 1
        return 6 + rr_b[0]
    rr_g = [0]
    rr_m = [0]

    def psg():
        rr_g[0] = (rr_g[0] + 1) % 6
        return rr_g[0]

    def psm():
        return 6

    def T(eng, fn, r=(), w=(), **kw):
        return S.add(eng, fn, reads=r, writes=w, **kw)

    def mm(out, lhsT, rhs, r, w, start=True, stop=True):
        T("pe", lambda e: e.matmul(out, lhsT=lhsT, rhs=rhs, start=start, stop=stop), r, w)

    def actf(out, in_, func, r, w, scale=1.0, bias=None):
        if bias is None:
            T("act", lambda e: e.activation(out=out, in_=in_, func=func, scale=scale), r, w)
        else:
            T("act", lambda e: e.activation(out=out, in_=in_, func=func, scale=scale, bias=bias), r, w)

    def tt(eng, out, in0, in1, op, r, w):
        T(eng, lambda e: e.tensor_tensor(out=out, in0=in0, in1=in1, op=op), r, w)

    def ts(eng, out, in0, s1, op0, r, w, s2=None, op1=None):
        if op1 is None:
            T(eng, lambda e: e.tensor_scalar(out=out, in0=in0, scalar1=s1, scalar2=None, op0=op0), r, w)
        else:
            T(eng, lambda e: e.tensor_scalar(out=out, in0=in0, scalar1=s1, scalar2=s2, op0=op0, op1=op1), r, w)

    def stt(eng, out, in0, sc, in1, op0, op1, r, w):
        T(eng, lambda e: e.scalar_tensor_tensor(out=out, in0=in0, scalar=sc, in1=in1, op0=op0, op1=op1), r, w)

    def cp(eng, out, in_, r, w):
        if eng == "act":
            T("act", lambda e: e.activation(out=out, in_=in_, func=AF.Copy), r, w)
        else:
            T(eng, lambda e: e.tensor_copy(out=out, in_=in_), r, w)

    def mset(eng, ap, v, w):
        T(eng, lambda e: e.memset(ap, v), (), w)

    def dma(eng, out, in_, r, w, chan):
        T(eng, lambda e: [e.dma_start(out=out, in_=in_)], r, w, chan=chan, ndma=1)

    ring = [sb([128, 1024], BF16) for _ in range(RING)]
    oaT = sb([128, 4, SEQ], BF16)
    ident = sb([128, 128], BF16)
    ones = sb([128, 128], BF16)
    tri = sb([128, 128], BF16)
    gt = sb([128, 32], F32)
    gfin = gt[:, 21:29]
    att_base = top[0]
    Ptp = [sb([128, 2 * TC], BF16) for _ in range(4)]
    ofs = [sb([128, TC], F32) for _ in range(3)]
    DEF = []
    itc = [0]
    ofn = [0]

    def run_deferred(force=False):
        while DEF and (force or DEF[0][0] <= itc[0]):
            DEF.pop(0)[1]()
    rrfs = [sb([128, TC], BF16) for _ in range(3)]
    free_slots = [0, 1, 2]

    def slot():
        return free_slots.pop(0)

    def release(sl_):
        free_slots.append(sl_)
    stg = [sb([128, TC], BF16) for _ in range(2)]
    glob_top = top[0]

    class WM:
        pos = 0
        issued = 0

        def issue(self, i):
            slot = i % RING
            sl = seq[i]
            dma("sp", ring[slot][:], wscr[sl], [("scr", sl)], [("ring", slot)], f"ring{slot}")

        def get(self, sl):
            i = self.pos
            self.pos += 1
            if seq is None:
                rec.append(sl)
                slot = i % RING
                dma("sp", ring[slot][:], wscr[sl], [("scr", sl)], [("ring", slot)], f"ring{slot}")
                return slot
            assert seq[i] == sl, (i, seq[i], sl)
            while self.issued < min(len(seq), i + RING - 3):
                self.issue(self.issued)
                self.issued += 1
            return i % RING

    W = WM()

    mset("pool", ident[:], 1.0, ["ident"])
    T("pool", lambda e: e.affine_select(out=ident[:], in_=ident[:], pattern=[[-1, 128]], compare_op=ALU.is_equal,
                                        fill=0.0, base=0, channel_multiplier=1), ["ident"], ["ident"])
    mset("pool", ones[:], 1.0, ["ones"])
    dma("pool", tri[:], c_tri, [], ["tri"], "c0")
    dma("sp", gt[:], c_gains, [], ["gfin", "gains"], "c1")

    class WPrep:
        order = None
        pos = 0

        def step(self, n):
            for _ in range(n):
                if self.pos >= len(self.order):
                    return
                s_ = self.order[self.pos]
                ch = self.pos % 8
                self.pos += 1
                dma("pool", wscr[s_], wsrc[s_], [], [("scr", s_), ("wpch", ch)], f"wp{ch}")

    WP = WPrep()
    if seq is None:
        WP.order = list(range(NSLAB))
    else:
        a1s = [SL_Q, SL_Q + 1, SL_K, SL_K + 1, SL_V, SL_V + 1, SL_V + 2, SL_V + 3, SL_Q + 2, SL_Q + 3, SL_K + 2, SL_K + 3]
        seen, o = set(a1s), list(a1s)
        for s_ in seq:
            if s_ not in seen:
                seen.add(s_)
                o.append(s_)
        WP.order = o

    Bf = {}

    def alloc_chunk(nxn=1, nxs=1):
        Bf["xs"] = [sb([128, 8, TC], F32) for _ in range(nxs)]
        Bf["xq"] = sb([128, 8, TC], BF16)
        Bf["xn"] = [sb([128, 8, TC], BF16) for _ in range(nxn)]
        Bf["sd"] = sb([128, TC], F32)
        Bf["rstd"] = sb([128, TC], F32)

    def ln_dma(c):
        i = c % len(Bf["xs"])
        xs = Bf["xs"][i]
        cs = slice(c * TC, (c + 1) * TC)
        dma("sp", xs[:], xT3[:, :, cs], [], [("xs", i, k) for k in range(8)], f"xs{i}")

    def ln_sq(c):
        i = c % len(Bf["xs"])
        xs, xq = Bf["xs"][i], Bf["xq"]
        T("act", lambda e: e.activation(out=xq[:], in_=xs[:], func=AF.Square), [("xs", i, k) for k in range(8)], ["xq"])

    def ln_load(c):
        ln_dma(c)
        ln_sq(c)

    def ln_ss(c):
        xq, sd, rstd = Bf["xq"], Bf["sd"], Bf["rstd"]
        pb = psm()
        for kt in range(8):
            mm(ps[pb][:], ones[:], xq[:, kt, :], ["xq", "ones"], [("ps", pb)], start=(kt == 0), stop=(kt == 7))
        actf(sd[:], ps[pb][:], AF.Ln, [("ps", pb)], ["sd"], scale=1.0 / 1024, bias=EPS)
        actf(rstd[:], sd[:], AF.Exp, ["sd"], ["rstd"], scale=-0.5)

    def ln_xn(c, j):
        i = c % len(Bf["xs"])
        xs, xn, rstd = Bf["xs"][i], Bf["xn"], Bf["rstd"]
        for kt in range(8):
            stt("dve", xn[j][:, kt, :], xs[:, kt, :], gt[:, kt:kt + 1], rstd[:], ALU.mult, ALU.mult,
                [("xs", i, kt), "rstd", "gains"], [("xn", j, kt)])

    def load_norm(c, j):
        ln_load(c)
        ln_ss(c)
        ln_xn(c, j)

    def rms_fm(src, nt, nfeat, dstf, key_src, key_dst, g0):
        xq, sd, rstd = Bf["xq"], Bf["sd"], Bf["rstd"]
        T("act", lambda e: e.activation(out=xq[:, 0:nt, :], in_=src[:, 0:nt, :], func=AF.Square), [key_src(k) for k in range(nt)], ["xq"])
        pb = psm()
        for kt in range(nt):
            mm(ps[pb][:], ones[:], xq[:, kt, :], ["xq", "ones"], [("ps", pb)], start=(kt == 0), stop=(kt == nt - 1))
        actf(sd[:], ps[pb][:], AF.Ln, [("ps", pb)], ["sd"], scale=1.0 / nfeat, bias=EPS)
        actf(rstd[:], sd[:], AF.Exp, ["sd"], ["rstd"], scale=-0.5)
        for kt in range(nt):
            stt("dve", dstf(kt), src[:, kt, :], gt[:, g0 + kt:g0 + kt + 1], rstd[:], ALU.mult, ALU.mult, [key_src(kt), "rstd", "gains"], [key_dst(kt)])

    o_banks = [6, 7]

    def attention(tiles, lhsT_of, rhs_of, vx_of, exp_scale, kkeys, qkeys, vkeys, finish, side=()):
        flat = []
        for c in range(NCH):
            tl = tiles(c)
            for i, (kt, lo, masks) in enumerate(tl):
                flat.append((c, kt, lo, masks, i == 0, i == len(tl) - 1))
        pairs = [flat[i:i + 2] for i in range(0, len(flat), 2)]
        side = list(side)
        per = (len(side) + len(pairs) - 1) // max(1, len(pairs))
        slots = {}

        def qk(j):
            pd = slot()
            slots[j] = pd
            for u, (c, kt, lo, masks, first, last) in enumerate(pairs[j]):
                o = 512 * u
                mm(psd[pd][:, o + lo:o + TC], lhsT_of(kt), rhs_of(c, lo), kkeys + qkeys(c), [("psd", pd, u)], start=True, stop=(len(masks) == 0))
                for mi, (c0, n, mAP, mk) in enumerate(masks):
                    mm(psd[pd][:, o + c0:o + c0 + n], ident[:], mAP, ["ident"] + mk, [("psd", pd, u)], start=False, stop=(mi == len(masks) - 1))

        qk(0)
        if len(pairs) > 1:
            qk(1)
        for j in range(len(pairs)):
            if j + 2 < len(pairs):
                qk(j + 2)
            pd, pt = slots[j], j % 4
            pr = pairs[j]
            if len(pr) == 2 and pr[1][2] == 0:
                lo0 = pr[0][2]
                actf(Ptp[pt][:, lo0:2 * TC], psd[pd][:, lo0:2 * TC], AF.Exp, [("psd", pd, 0), ("psd", pd, 1)], [("Pt", pt, 0), ("Pt", pt, 1)], scale=exp_scale)
            else:
                for u, (c, kt, lo, masks, first, last) in enumerate(pr):
                    o = 512 * u
                    actf(Ptp[pt][:, o + lo:o + TC], psd[pd][:, o + lo:o + TC], AF.Exp, [("psd", pd, u)], [("Pt", pt, u)], scale=exp_scale)
            release(pd)
            for u, (c, kt, lo, masks, first, last) in enumerate(pr):
                o = 512 * u
                ob = o_banks[c % 2]
                mm(ps[ob][0:96, lo:TC], vx_of(kt), Ptp[pt][:, o + lo:o + TC], [("Pt", pt, u)] + vkeys(kt), [("ps", ob)], start=first, stop=last)
                if last:
                    finish(c, ob)
            for _ in range(per):
                if side:
                    side.pop(0)()
            itc[0] += 1
            run_deferred()
        while side:
            side.pop(0)()

    def make_finish(dstT, h):
        f, hh = h // 2, h % 2

        def finish(c, ob):
            cs = slice(c * TC, (c + 1) * TC)
            oi = ofn[0] % 3
            ofn[0] += 1
            of = ofs[oi]
            ok = ("ofs", oi)
            cp("dve", of[0:65, :], ps[ob][0:65, :], [("ps", ob)], [ok])
            rrf = rrfs[oi]
            rk = ("rrf", oi)
            T("dve", lambda e: e.reciprocal(out=rrf[64:65, :], in_=of[64:65, :]), [ok], [rk])

            def stage_b():
                pd = slot()
                mm(psd[pd][0:64, 0:TC], ones[64:65, 0:64], rrf[64:65, :], [rk, "ones"], [("psd", pd, 0)])
                if hh == 0:
                    tt("dve", dstT[0:64, f, cs], of[0:64, :], psd[pd][0:64, 0:TC], ALU.mult, [ok, ("psd", pd, 0)], [("oT", id(dstT), f, c, 0)])
                else:
                    sg = stg[c % 2]
                    tt("dve", sg[0:64, :], of[0:64, :], psd[pd][0:64, 0:TC], ALU.mult, [ok, ("psd", pd, 0)], [("stg", c % 2)])
                    dma("sp", dstT[64:128, f, cs], sg[0:64, :], [("stg", c % 2)], [("oT", id(dstT), f, c, 1)], f"stg{c % 2}")
                release(pd)
            DEF.append((itc[0] + 7, stage_b))
        return finish

    oT_keys = lambda dstT, c: [("oT", id(dstT), f, c, hh) for f in range(4) for hh in range(2)]

    def bc16(ap2, n):
        return ap2.unsqueeze(2).to_broadcast([128, 16, n])

    for g in range(2):
        top[0] = glob_top
        qT = sb([128, 2, SEQ], BF16)
        kT = sb([128, 2, SEQ], BF16)
        Vx = sb([128, 32, 4, 96], BF16)
        ksum = sb([128, 2, 16], F32)
        KM = sb([128, 2, 32], BF16)
        M16 = sb([128, 16, 16], F32)
        Mb = [sb([128, 256], BF16) for _ in range(2)]
        gs = sb([128, 16, 16], F32)
        g1 = sb([128, 16, 16], F32)
        eq = sb([128, 16, 16], F32)
        mx = sb([128, 16], F32)
        MT = sb([128, SEQ], BF16)
        a1_top = top[0]
        wres = [sb([128, 1024], BF16) for _ in range(4)]
        alloc_chunk(2, 2)
        xn = Bf["xn"]
        mset("pool", Vx[:, :, :, 64:96], 1.0, [("Vx", t_) for t_ in range(32)])
        mset("pool", KM[:], 0.0, ["KM"])
        if g == 0:
            WP.step(8)
        for i_, sl_ in enumerate([SL_Q + 2 * g, SL_Q + 2 * g + 1, SL_K + 2 * g, SL_K + 2 * g + 1]):
            dma("sp", wres[i_][:], wscr[sl_], [("scr", sl_)], [("wres", i_)], f"wres{i_}")

        def gate_part1(c, qT=qT, KM=KM, M16=M16, Mb=Mb, gs=gs, g1=g1, eq=eq, mx=mx, ksum=ksum):
            for fl in range(2):
                ts("dve", KM[0:64, fl, 2 * c:2 * c + 2], ksum[0:64, fl, 2 * c:2 * c + 2], 1.0 / 256, ALU.mult, ["ksum"], ["KM"])
                ts("dve", KM[64:128, fl, 16 + 2 * c:18 + 2 * c], ksum[64:128, fl, 2 * c:2 * c + 2], 1.0 / 256, ALU.mult, ["ksum"], ["KM"])
            qlo, qhi = 2 * c, 2 * c + 1
            mset("pool", M16[:], NEG, ["M16"])
            if qhi <= 3:
                mset("pool", M16[:, 0:8, 0:qlo + 1], 0.0, ["M16"])
                mset("pool", M16[:, 8:16, 0:qhi + 1], 0.0, ["M16"])
            else:
                pb = psm()
                for t4 in range(4):
                    t = 4 * c + t4
                    for fl in range(2):
                        mm(ps[pb][:, t4 * 64 + fl * 32:t4 * 64 + (fl + 1) * 32], qT[:, fl, t * 128:(t + 1) * 128], KM[:, fl, :],
                           [("qT", fl, c), "KM"], [("ps", pb)])
                g3 = ps[pb][:, 0:256].rearrange("p (g n) -> p g n", g=16)[:, :, 0:qhi]
                cp("dve", gs[:, :, 0:qhi], g3, [("ps", pb)], ["gs"])
                T("dve", lambda e: e.memset(gs[:, 0:8, qlo:qlo + 1], -1.0e9), (), ["gs"])
                src = gs
                for rnd in range(3):
                    T("dve", lambda e, src=src: e.reduce_max(out=mx[:], in_=src[:, :, 0:qhi], axis=AX.X), ["gs", "g1"], ["mx"])
                    if rnd < 2:
                        tt("dve", eq[:, :, 0:qhi], src[:, :, 0:qhi], bc16(mx[:], qhi), ALU.is_ge, ["gs", "g1", "mx"], ["eq"])
                        stt("dve", g1[:, :, 0:qhi], eq[:, :, 0:qhi], -1.0e9, src[:, :, 0:qhi], ALU.mult, ALU.add, ["eq", "gs", "g1"], ["g1"])
                        src = g1
                tt("dve", eq[:, :, 0:qhi], gs[:, :, 0:qhi], bc16(mx[:], qhi), ALU.is_ge, ["gs", "mx"], ["eq"])
                ts("dve", M16[:, :, 0:qhi], eq[:, :, 0:qhi], -NEG, ALU.mult, ["eq"], ["M16"], s2=NEG, op1=ALU.add)
                T("dve", lambda e: e.memset(M16[:, 0:8, qlo:qlo + 1], 0.0), (), ["M16"])
                T("dve", lambda e: e.memset(M16[:, 8:16, qhi:qhi + 1], 0.0), (), ["M16"])
            cp("pool", Mb[c % 2][:], M16[:].rearrange("p g n -> p (g n)"), ["M16"], [("Mb", c % 2)])

        def gate_part2(c, Mb=Mb, MT=MT):
            for t4 in range(4):
                T("pe", lambda e, t4=t4, Mb=Mb: e.transpose(psT[0:64, t4 * 128:(t4 + 1) * 128], Mb[c % 2][:, t4 * 64:(t4 + 1) * 64], ident[:]),
                  [("Mb", c % 2), "ident"], [("ps", 7)])
            cp("act", MT[0:64, c * TC:(c + 1) * TC], psT[0:64, 0:TC], [("ps", 7)], [("MT", c)])

        ln_load(0)
        ln_ss(0)
        ln_xn(0, 0)
        ln_dma(1)
        for c in range(NCH):
            j = c % 2
            cs = slice(c * TC, (c + 1) * TC)
            if c + 2 < NCH:
                ln_dma(c + 2)
            if g == 0:
                WP.step(1)
            for fl in range(2):
                pb = psg()
                for kt in range(8):
                    mm(ps[pb][:], wres[fl][:, kt * 128:(kt + 1) * 128], xn[j][:, kt, :], [("wres", fl), ("xn", j, kt)], [("ps", pb)],
                       start=(kt == 0), stop=(kt == 7))
                actf(qT[:, fl, cs], ps[pb][:], AF.Identity, [("ps", pb)], [("qT", fl, c)], scale=0.125)
            if c + 1 < NCH:
                ln_sq(c + 1)
            if c > 0:
                gate_part2(c - 1)
            for fl in range(2):
                pb = psg()
                for kt in range(8):
                    mm(ps[pb][:], wres[2 + fl][:, kt * 128:(kt + 1) * 128], xn[j][:, kt, :], [("wres", 2 + fl), ("xn", j, kt)], [("ps", pb)],
                       start=(kt == 0), stop=(kt == 7))
                cp("act", kT[:, fl, cs], ps[pb][:], [("ps", pb)], [("kT", fl, c)])
                T("dve", lambda e, fl=fl, c=c, cs=cs, kT=kT, ksum=ksum: e.reduce_sum(
                    out=ksum[:, fl, 2 * c:2 * c + 2], in_=kT[:, fl, cs].rearrange("p (b t) -> p b t", b=2), axis=AX.X),
                  [("kT", fl, c)], ["ksum"])
            if c + 1 < NCH:
                ln_ss(c + 1)
            pbs = [psg() for _ in range(4)]
            for jv in range(4):
                sl = W.get(SL_V + jv)
                for t4 in range(4):
                    for kk in range(2):
                        kt = 2 * jv + kk
                        mm(ps[pbs[t4]][:, 0:256], xn[j][:, kt, t4 * 128:(t4 + 1) * 128],
                           ring[sl][:, kk * 512 + g * 256:kk * 512 + (g + 1) * 256],
                           [("ring", sl), ("xn", j, kt)], [("ps", pbs[t4])], start=(kt == 0), stop=(kt == 7))
            if c + 1 < NCH:
                ln_xn(c + 1, (c + 1) % 2)
            for t4 in range(4):
                cp("act", Vx[:, 4 * c + t4, :, 0:64], ps[pbs[t4]][:, 0:256].rearrange("p (h d) -> p h d", h=4),
                   [("ps", pbs[t4])], [("Vx", 4 * c + t4)])
            gate_part1(c)
        gate_part2(NCH - 1)
        S.barrier()

        top[0] = a1_top
        Qaug = [sb([128, SEQ], BF16) for _ in range(2)]
        Kaug = [sb([128, SEQ], BF16) for _ in range(2)]
        Dtmp = sb([128, 8 * 256], F32)
        caus = sb([128, 256], F32)
        b31 = sb([128, 8], F32)
        Dm = sb([128, 8, 256], BF16)
        dma("sp", Dtmp[:], c_Tm, [], ["Dtmp"], "c3")
        dma("sp", caus[:], c_caus, [], ["caus"], "c4")
        dma("sp", b31[:], c_b31, [], ["b31"], "c5")
        for h in range(8):
            stt("dve", Dm[:, h, :], Dtmp[:, h * 256:(h + 1) * 256], b31[:, h:h + 1], caus[:], ALU.subtract, ALU.add,
                ["Dtmp", "b31", "caus"], [("Dm", h)])
        for b in range(2):
            mset("pool", Qaug[b][64:128, :], 0.0, [("Qaug", b, "m")])
            mset("pool", Kaug[b][64:128, :], 0.0, [("Kaug", b, "e")])
            dma("pool", Kaug[b][64:80, :], c_E, [], [("Kaug", b, "e")], f"ce{b}")

        def moba_tiles(h):
            def tiles(c):
                out = []
                for kt in range(4 * c + 4):
                    jj = kt - 4 * c
                    lo = max(0, 128 * jj)
                    masks = []
                    if jj == -1:
                        masks = [(0, 128, Dm[:, h, 128:256], [("Dm", h)])]
                    elif jj >= 0:
                        n = min(256, TC - 128 * jj)
                        masks = [(128 * jj, n, Dm[:, h, 0:n], [("Dm", h)])]
                    out.append((kt, lo, masks))
                return out
            return tiles

        def head_copies(hl):
            b, fl, hh = hl % 2, hl // 2, hl % 2
            dma("sp", Qaug[b][0:64, :], qT[64 * hh:64 * hh + 64, fl, :], [("qT", fl, c) for c in range(NCH)], [("Qaug", b, "q")], f"qa{b}")
            dma("sp", Kaug[b][0:64, :], kT[64 * hh:64 * hh + 64, fl, :], [("kT", fl, c) for c in range(NCH)], [("Kaug", b, "k")], f"ka{b}")
            dma("sp", Qaug[b][64:80, :], MT[16 * hl:16 * hl + 16, :], [("MT", i) for i in range(8)], [("Qaug", b, "m")], f"qm{b}")

        head_copies(0)
        for hl in range(4):
            h = 4 * g + hl
            b = hl % 2
            side = []
            if hl + 1 < 4:
                side.append(lambda hl=hl: head_copies(hl + 1))
            side += [(lambda: WP.step(1)) for _ in range(40)]
            attention(moba_tiles(h),
                      lambda kt, b=b, Kaug=Kaug: Kaug[b][:, kt * 128:(kt + 1) * 128],
                      lambda c, lo, b=b, Qaug=Qaug: Qaug[b][:, c * TC + lo:(c + 1) * TC],
                      lambda kt, hl=hl, Vx=Vx: Vx[:, kt, hl, :],
                      1.0,
                      [("Kaug", b, "k"), ("Kaug", b, "e")],
                      lambda c, b=b: [("Qaug", b, "q"), ("Qaug", b, "m")],
                      lambda kt: [("Vx", kt)],
                      make_finish(oaT, h), side)
        run_deferred(True)
        S.barrier()

    top[0] = glob_top
    obT = sb([128, 4, SEQ], BF16)
    cqnT = sb([128, 3, SEQ], BF16)
    ckvnT = sb([128, 2, SEQ], BF16)
    krT = sb([128, SEQ], BF16)
    mla_top = top[0]
    alloc_chunk()
    xn = Bf["xn"]
    cqf = sb([128, 3, TC], F32)
    ckvf = sb([128, 2, TC], F32)
    cst = [sb([128, TC], F32) for _ in range(1)] * 2
    snt = [sb([128, TC], F32) for _ in range(1)] * 2
    tabmod = [1]
    t1 = sb([128, TC], F32)
    t2 = sb([128, TC], F32)
    a2_top = top[0]
    top[0] = mla_top
    cst2 = [sb([128, TC], F32) for _ in range(2)]
    snt2 = [sb([128, TC], F32) for _ in range(2)]
    t1b = sb([128, TC], F32)
    t2b = sb([128, TC], F32)
    KT = [sb([128, SEQ], BF16) for _ in range(2)]
    QTh = [sb([128, SEQ], BF16) for _ in range(2)]
    Vh = [sb([128, 32, 96], BF16) for _ in range(2)]

    tabn = [0]

    def load_tables(c):
        i = tabn[0] % tabmod[0]
        tabn[0] += 1
        cs = slice(c * TC, (c + 1) * TC)
        dma("sp", cst[i][64:96, :], c_cos[:, cs], [], [("cst", i)], f"cst{i}")
        dma("sp", snt[i][64:96, :], c_sin[:, cs], [], [("snt", i)], f"snt{i}")
        return i

    def rope(dst, p1, p2, ti, rkeys, wkeys):
        tt("dve", t1[64:96, :], p1, cst[ti][64:96, :], ALU.mult, rkeys[0:1] + [("cst", ti)], ["t1"])
        tt("dve", t2[64:96, :], p2, snt[ti][64:96, :], ALU.mult, rkeys[1:2] + [("snt", ti)], ["t2"])
        tt("pool", dst, t1[64:96, :], t2[64:96, :], ALU.add, ["t1", "t2"], wkeys)

    ln_load(0)
    ln_ss(0)
    ln_xn(0, 0)
    for c in range(NCH):
        j = 0
        cs = slice(c * TC, (c + 1) * TC)
        if c + 1 < NCH:
            ln_load(c + 1)
        ti = load_tables(c)
        for f in range(3):
            sl = W.get(SL_CQ + f)
            pb = psg()
            for kt in range(8):
                mm(ps[pb][:], ring[sl][:, kt * 128:(kt + 1) * 128], xn[j][:, kt, :], [("ring", sl), ("xn", j, kt)], [("ps", pb)],
                   start=(kt == 0), stop=(kt == 7))
            cp("act", cqf[:, f, :], ps[pb][:], [("ps", pb)], ["cqf"])
        for f in range(2):
            sl = W.get(SL_CKV + f)
            pb = psg()
            for kt in range(8):
                mm(ps[pb][:], ring[sl][:, kt * 128:(kt + 1) * 128], xn[j][:, kt, :], [("ring", sl), ("xn", j, kt)], [("ps", pb)],
                   start=(kt == 0), stop=(kt == 7))
            cp("dve", ckvf[:, f, :], ps[pb][:], [("ps", pb)], ["ckvf"])
        pk = []
        for i in range(2):
            sl = W.get(SL_KR + i)
            pb = psg()
            pk.append(pb)
            for kt in range(8):
                mm(ps[pb][0:96, :], ring[sl][:, kt * 128:kt * 128 + 96], xn[j][:, kt, :], [("ring", sl), ("xn", j, kt)], [("ps", pb)],
                   start=(kt == 0), stop=(kt == 7))
        rope(krT[64:96, cs], ps[pk[0]][64:96, :], ps[pk[1]][64:96, :], ti, [("ps", pk[0]), ("ps", pk[1])], [("krT", c)])
        rms_fm(cqf, 3, 384.0, lambda kt, cs=cs: cqnT[:, kt, cs], lambda k: "cqf", lambda kt, c=c: ("cqnT", c), 16)
        if c + 1 < NCH:
            ln_ss(c + 1)
            ln_xn(c + 1, 0)
        rms_fm(ckvf, 2, 256.0, lambda kt, cs=cs: ckvnT[:, kt, cs], lambda k: "ckvf", lambda kt, c=c: ("ckvnT", c), 19)
    S.barrier()
    cst[0], cst[1], snt[0], snt[1] = cst2[0], cst2[1], snt2[0], snt2[1]
    tabmod[0] = 2
    t1, t2 = t1b, t2b

    WP.step(NSLAB)
    for b in range(2):
        mset("pool", KT[b][96:128, :], 0.0, [("KT", b, "z")])
        mset("pool", QTh[b][96:128, :], 0.0, [("QT", b, "z")])
        mset("pool", Vh[b][:, :, 64:96], 1.0, [("Vh", b, g_) for g_ in range(4)])

    def mla_tiles(c):
        out = []
        for kt in range(4 * c + 4):
            jj = kt - 4 * c
            lo = max(0, 128 * jj)
            masks = [(128 * jj, 128, tri[:], ["tri"])] if jj >= 0 else []
            out.append((kt, lo, masks))
        return out

    def rope_dve(dst, p1, p2, ti, rkeys, wkeys):
        tt("dve", t1[64:96, :], p1, cst[ti][64:96, :], ALU.mult, rkeys[0:1] + [("cst", ti)], ["t1"])
        tt("dve", t2[64:96, :], p2, snt[ti][64:96, :], ALU.mult, rkeys[1:2] + [("snt", ti)], ["t2"])
        tt("dve", dst, t1[64:96, :], t2[64:96, :], ALU.add, ["t1", "t2"], wkeys)

    def mla_build(h):
        b = h % 2
        st = {}
        out = []

        def get_slabs():
            st["sl"] = W.get(SL_HEAD + h)
            st["slv"] = W.get(SL_UV)
        out.append(get_slabs)

        def kbuild(c):
            sl = st["sl"]
            cs = slice(c * TC, (c + 1) * TC)
            pd = slot()
            for kt in range(2):
                mm(psd[pd][0:64, 0:TC], ring[sl][:, (6 + kt) * 128:(6 + kt) * 128 + 64], ckvnT[:, kt, cs], [("ring", sl), ("ckvnT", c)], [("psd", pd, 0)],
                   start=(kt == 0), stop=(kt == 1))
            cp("act", KT[b][0:64, cs], psd[pd][0:64, 0:TC], [("psd", pd, 0)], [("KT", b, "n", c)])
            release(pd)
            cp("pool", KT[b][64:96, cs], krT[64:96, cs], [("krT", c)], [("KT", b, "r", c)])

        def vbuild(g_):
            slv = st["slv"]
            pd = slot()
            for i in range(8):
                t = 8 * g_ + i
                for kt in range(2):
                    mm(psd[pd][:, i * 64:(i + 1) * 64], ckvnT[:, kt, t * 128:(t + 1) * 128], ring[slv][:, kt * 512 + h * 64:kt * 512 + h * 64 + 64],
                       [("ring", slv), ("ckvnT", t // 4)], [("psd", pd, 0)], start=(kt == 0), stop=(kt == 1))
            cp("act", Vh[b][:, 8 * g_:8 * g_ + 8, 0:64], psd[pd][:, 0:TC].rearrange("p (i d) -> p i d", i=8), [("psd", pd, 0)], [("Vh", b, g_)])
            release(pd)

        def qbuild(c):
            sl = st["sl"]
            cs = slice(c * TC, (c + 1) * TC)
            ti = load_tables(c)
            pd = slot()
            for kt in range(3):
                mm(psd[pd][0:96, 0:TC], ring[sl][:, kt * 128:kt * 128 + 96], cqnT[:, kt, cs], [("ring", sl), ("cqnT", c)], [("psd", pd, 0)],
                   start=(kt == 0), stop=(kt == 2))
            for kt in range(3):
                mm(psd[pd][0:96, TC:2 * TC], ring[sl][:, (3 + kt) * 128:(3 + kt) * 128 + 96], cqnT[:, kt, cs], [("ring", sl), ("cqnT", c)], [("psd", pd, 1)],
                   start=(kt == 0), stop=(kt == 2))
            cp("act", QTh[b][0:64, cs], psd[pd][0:64, 0:TC], [("psd", pd, 0)], [("QT", b, "n", c)])
            rope_dve(QTh[b][64:96, cs], psd[pd][64:96, 0:TC], psd[pd][64:96, TC:2 * TC], ti, [("psd", pd, 0), ("psd", pd, 1)], [("QT", b, "r", c)])
            release(pd)

        for c in range(NCH):
            out.append(lambda c=c: kbuild(c))
        for g_ in range(4):
            out.append(lambda g_=g_: vbuild(g_))
        for c in range(NCH):
            out.append(lambda c=c: qbuild(c))
        return out

    for fn in mla_build(0):
        fn()
    for h in range(8):
        b = h % 2
        side = []
        if h + 1 < 8:
            for fn in mla_build(h + 1):
                fn()
        attention(mla_tiles,
                  lambda kt, b=b: KT[b][:, kt * 128:(kt + 1) * 128],
                  lambda c, lo, b=b: QTh[b][:, c * TC + lo:(c + 1) * TC],
                  lambda kt, b=b: Vh[b][:, kt, :],
                  96.0 ** -0.5,
                  [("KT", b, "z")] + [("KT", b, "n", c) for c in range(NCH)] + [("KT", b, "r", c) for c in range(NCH)],
                  lambda c, b=b: [("QT", b, "n", c), ("QT", b, "r", c), ("QT", b, "z")],
                  lambda kt, b=b: [("Vh", b, kt // 8)],
                  make_finish(obT, h), side)
    run_deferred(True)
    S.barrier()
    if DEBUG:
        dma("sp", dbg_oa, oaT[:].rearrange("p a t -> p (a t)"), [], [], "dbg")
        dma("sp", dbg_ob, obT[:].rearrange("p a t -> p (a t)"), [], [], "dbg")
        S.barrier()
    top[0] = glob_top
    obT_c = sb([128, 4, SEQ], BF16)
    xs2 = [sb([128, 8, TC], F32) for _ in range(2)]
    xq = sb([128, 8, TC], BF16)
    xnc = sb([128, 8, TC], BF16)
    sd = sb([128, TC], F32)
    rstd = sb([128, TC], F32)
    r2 = sb([128, TC], F32)
    mg = sb([128, 8, TC], BF16)
    mnb = sb([128, 8, TC], BF16)
    upT = sb([128, 32, TC], BF16)
    _cur = top[0]
    top[0] = att_base
    sga = sb([128, TC], F32)
    sgb = sb([128, TC], F32)
    ta = sb([128, TC], F32)
    tb_ = sb([128, TC], F32)
    rl = [sb([128, TC], F32) for _ in range(2)]
    assert top[0] <= glob_top
    top[0] = _cur

    def xk(i):
        return [("xs", i, k) for k in range(8)]

    def c_load(c):
        i = c % 2
        dma("sp", xs2[i][:], xT3[:, :, c * TC:(c + 1) * TC], [], xk(i), f"xs{i}")

    def c_norm_a(c):
        i = c % 2
        T("act", lambda e: e.activation(out=xq[:], in_=xs2[i][:], func=AF.Square), xk(i), ["xq"])

    def c_norm_b(c):
        i = c % 2
        for kt in range(8):
            mm(ps[6][:], ones[:], xq[:, kt, :], ["xq", "ones"], [("ps", 6)], start=(kt == 0), stop=(kt == 7))
        actf(sd[:], ps[6][:], AF.Ln, [("ps", 6)], ["sd"], scale=1.0 / 1024, bias=EPS)
        actf(rstd[:], sd[:], AF.Exp, ["sd"], ["rstd"], scale=-0.5)
        for kt in range(8):
            stt("dve", xnc[:, kt, :], xs2[i][:, kt, :], gt[:, kt:kt + 1], rstd[:], ALU.mult, ALU.mult,
                [("xs", i, kt), "rstd", "gains"], [("xn", kt)])

    c_load(0)
    c_norm_a(0)
    c_norm_b(0)
    for c in range(NCH):
        cs = slice(c * TC, (c + 1) * TC)
        i = c % 2
        xs = xs2[i]
        if c + 1 < NCH:
            c_load(c + 1)
        for f in range(8):
            sa, sbb, sp_ = W.get(SL_G + f), W.get(SL_G + 8 + f), W.get(SL_P + f)
            pga, pgb, ppa, ppb = psg(), psg(), psg(), psg()
            for kt in range(8):
                mm(ps[pga][:], ring[sa][:, kt * 128:(kt + 1) * 128], xnc[:, kt, :], [("ring", sa), ("xn", kt)], [("ps", pga)],
                   start=(kt == 0), stop=(kt == 7))
            for kt in range(8):
                mm(ps[pgb][:], ring[sbb][:, kt * 128:(kt + 1) * 128], xnc[:, kt, :], [("ring", sbb), ("xn", kt)], [("ps", pgb)],
                   start=(kt == 0), stop=(kt == 7))
            for kt in range(4):
                mm(ps[ppa][:], ring[sp_][:, kt * 128:(kt + 1) * 128], oaT[:, kt, cs], [("ring", sp_)] + oT_keys(oaT, c), [("ps", ppa)],
                   start=(kt == 0), stop=(kt == 3))
            for kt in range(4):
                mm(ps[ppb][:], ring[sp_][:, (4 + kt) * 128:(5 + kt) * 128], obT[:, kt, cs], [("ring", sp_)] + oT_keys(obT, c), [("ps", ppb)],
                   start=(kt == 0), stop=(kt == 3))
            actf(sga[:], ps[pga][:], AF.Sigmoid, [("ps", pga)], ["sga"])
            actf(sgb[:], ps[pgb][:], AF.Sigmoid, [("ps", pgb)], ["sgb"])
            tt("dve", ta[:], sga[:], ps[ppa][:], ALU.mult, ["sga", ("ps", ppa)], ["ta"])
            tt("dve", tb_[:], sgb[:], ps[ppb][:], ALU.mult, ["sgb", ("ps", ppb)], ["tb"])
            tt("pool", mg[:, f, :], ta[:], tb_[:], ALU.add, ["ta", "tb"], [("mg", f)])
        for f in range(8):
            sl = W.get(SL_O + f)
            pb = psg()
            for kt in range(8):
                mm(ps[pb][:], ring[sl][:, kt * 128:(kt + 1) * 128], mg[:, kt, :], [("ring", sl), ("mg", kt)], [("ps", pb)],
                   start=(kt == 0), stop=(kt == 7))
            tt("dve", xs[:, f, :], xs[:, f, :], ps[pb][:], ALU.add, [("xs", i, f), ("ps", pb)], [("xs", i, f)])
            ts("dve", mnb[:, f, :], xs[:, f, :], gt[:, 8 + f:9 + f], ALU.mult, [("xs", i, f), "gains"], [("mnb", f)])
        if DEBUG:
            dma("sp", dbg_h[:, :, cs], xs[:], xk(i), [], "dbg")
            dma("sp", dbg_mg[:, :, cs], mg[:], [("mg", k) for k in range(8)], [], "dbg")
        T("act", lambda e, xs=xs: e.activation(out=xq[:], in_=xs[:], func=AF.Square), xk(i), ["xq"])
        for kt in range(8):
            mm(ps[6][:], ones[:], xq[:, kt, :], ["xq", "ones"], [("ps", 6)], start=(kt == 0), stop=(kt == 7))
        actf(sd[:], ps[6][:], AF.Ln, [("ps", 6)], ["sd"], scale=1.0 / 1024, bias=EPS)
        actf(r2[:], sd[:], AF.Exp, ["sd"], ["r2"], scale=-1.0)
        for ff in range(32):
            sl = W.get(SL_UP + ff)
            pb = psg()
            for kt in range(8):
                mm(ps[pb][:], ring[sl][:, kt * 128:(kt + 1) * 128], mnb[:, kt, :], [("ring", sl), ("mnb", kt)], [("ps", pb)],
                   start=(kt == 0), stop=(kt == 7))
            actf(rl[ff % 2][:], ps[pb][:], AF.Relu, [("ps", pb)], [("rl", ff % 2)])
            tt("pool", upT[:, ff, :], rl[ff % 2][:], rl[ff % 2][:], ALU.mult, [("rl", ff % 2)], [("upT", ff)])
            if c + 1 < NCH:
                if ff == 4:
                    c_norm_a(c + 1)
                if ff == 12:
                    c_norm_b(c + 1)
        for f in range(8):
            pb = psg()
            for jd in range(4):
                sl = W.get(SL_DN + f * 4 + jd)
                for i8 in range(8):
                    ff = 8 * jd + i8
                    mm(ps[pb][:], ring[sl][:, i8 * 128:(i8 + 1) * 128], upT[:, ff, :], [("ring", sl), ("upT", ff)], [("ps", pb)],
                       start=(ff == 0), stop=(ff == 31))
            tt("dve", ta[:], ps[pb][:], r2[:], ALU.mult, [("ps", pb), "r2"], ["ta"])
            tt("pool", xs[:, f, :], xs[:, f, :], ta[:], ALU.add, [("xs", i, f), "ta"], [("xs", i, f)])
        if DEBUG:
            dma("sp", dbg_h2[:, :, cs], xs[:], xk(i), [], "dbg")
        T("act", lambda e, xs=xs: e.activation(out=xq[:], in_=xs[:], func=AF.Square), xk(i), ["xq"])
        for kt in range(8):
            mm(ps[7][:], ones[:], xq[:, kt, :], ["xq", "ones"], [("ps", 7)], start=(kt == 0), stop=(kt == 7))
        actf(sd[:], ps[7][:], AF.Ln, [("ps", 7)], ["sd"], scale=1.0 / 1024, bias=EPS)
        actf(rstd[:], sd[:], AF.Exp, ["sd"], ["rstd"], scale=-0.5)
        for f in range(8):
            stt("dve", xs[:, f, :], xs[:, f, :], gt[:, 21 + f:22 + f], rstd[:], ALU.mult, ALU.mult,
                [("xs", i, f), "gfin", "rstd"], [("xs", i, f)])
        dma("pool", outT3[:, :, cs], xs[:], xk(i), [("out", c)], f"out{i}")

    if seq is None:
        return rec

    S.assign()
    sems = {s_: nc.alloc_semaphore("s_" + "_".join(map(str, s_))) for s_ in S.semnames()}
    final_out = [(("chan", f"out{i}"), S.chan_count[f"out{i}"]) for i in range(2)]

    def mk(ename):
        def f(h):
            S.emit_engine(ename, h, sems)
            if ename == "pool":
                for sname, v in final_out:
                    h.wait_ge(sems[sname], v)
        return f

    with nc.allow_low_precision(reason="bf16 matmul operands by design; fp32 accumulation"), nc.Block() as block:
        block.tensor(mk("pe"))
        block.scalar(mk("act"))
        block.vector(mk("dve"))
        block.gpsimd(mk("pool"))
        block.sync(mk("sp"))
    return nc


def _bucket(d):
    d = np.maximum(d, 0)
    dd = np.maximum(d, 1).astype(np.float32)
    large = 16 + (np.log(dd / np.float32(16)) / np.float32(math.log(128 / 16)) * np.float32(16)).astype(np.int32)
    large = np.minimum(large, 31)
    return np.where(d < 16, d, large)


def _host_layout(w_in, rel_bias, mla_q_norm, w_uq, mla_kv_norm, w_uk, w_uv, w_proj_a, w_proj_b, w_out,
                 norm_attn, norm_mlp, w_mlp_up, w_mlp_down, norm_final):
    f32 = np.float32
    w_in, w_uq, w_uk, w_uv = w_in[0], w_uq[0], w_uk[0], w_uv[0]
    wpa, wpb, wo, wup, wdn = w_proj_a[0], w_proj_b[0], w_out[0], w_mlp_up[0], w_mlp_down[0]
    ga, gm, gq, gkv = norm_attn[0], norm_mlp[0], mla_q_norm[0], mla_kv_norm[0]
    wsrc = np.zeros((NSLAB, 128, 1024), f32)

    def fm(s, Wm, f, nk, g=None, slot0=0, k0=0):
        for kt in range(nk):
            wsrc[s, :, (slot0 + kt) * 128:(slot0 + kt + 1) * 128] = Wm[(k0 + kt) * 128:(k0 + kt + 1) * 128, f * 128:(f + 1) * 128]

    for f in range(4):
        fm(SL_Q + f, w_in[:, 0:512], f, 8, ga)
        fm(SL_K + f, w_in[:, 512:1024], f, 8, ga)
    for j in range(4):
        for kk in range(2):
            kt = 2 * j + kk
            wsrc[SL_V + j, :, kk * 512:(kk + 1) * 512] = w_in[kt * 128:(kt + 1) * 128, 1024:1536]
    for f in range(3):
        fm(SL_CQ + f, w_in[:, 1536:1920], f, 8, ga)
    for f in range(2):
        fm(SL_CKV + f, w_in[:, 1920:2176], f, 8, ga)
    kr = w_in[:, 2176:2208]
    krs = np.concatenate([kr[:, 16:32], kr[:, 0:16]], axis=1)
    for kt in range(8):
        wsrc[SL_KR, :, kt * 128 + 64:kt * 128 + 96] = kr[kt * 128:(kt + 1) * 128]
        wsrc[SL_KR + 1, :, kt * 128 + 64:kt * 128 + 96] = krs[kt * 128:(kt + 1) * 128]
    for kt in range(2):
        wsrc[SL_UV, :, kt * 512:(kt + 1) * 512] = w_uv[kt * 128:(kt + 1) * 128]
    for h in range(8):
        s = SL_HEAD + h
        qh = w_uq[:, h * 96:(h + 1) * 96]
        qsw = np.concatenate([qh[:, 80:96], qh[:, 64:80]], axis=1)
        for kt in range(3):
            wsrc[s, :, kt * 128:kt * 128 + 96] = qh[kt * 128:(kt + 1) * 128]
            wsrc[s, :, (3 + kt) * 128 + 64:(3 + kt) * 128 + 96] = qsw[kt * 128:(kt + 1) * 128]
        for kt in range(2):
            wsrc[s, :, (6 + kt) * 128:(6 + kt) * 128 + 64] = w_uk[kt * 128:(kt + 1) * 128, h * 64:(h + 1) * 64]
    for f in range(16):
        fm(SL_G + f, w_in[:, 2208:4256], f, 8, ga)
    for f in range(8):
        fm(SL_P + f, wpa, f, 4)
        fm(SL_P + f, wpb, f, 4, slot0=4)
        fm(SL_O + f, wo, f, 8)
        for jd in range(4):
            fm(SL_DN + f * 4 + jd, wdn, f, 8, k0=8 * jd)
    for ff in range(32):
        fm(SL_UP + ff, wup, ff, 8, gm)
    c_gains = np.zeros((128, 32), f32)
    c_gains[:, 0:8] = ga.reshape(8, 128).T
    c_gains[:, 8:16] = gm.reshape(8, 128).T
    c_gains[:, 16:19] = gq.reshape(3, 128).T
    c_gains[:, 19:21] = gkv.reshape(2, 128).T
    c_gains[:, 21:29] = norm_final.reshape(8, 128).T

    p = np.arange(128)[:, None]
    c_tri = np.where(p > np.arange(128)[None, :], NEG, 0.0).astype(f32)
    c_E = (np.arange(SEQ)[None, :] // 256 == np.arange(16)[:, None]).astype(f32)
    jj = np.arange(256)[None, :]
    c_caus = np.where(jj < p, NEG, 0.0).astype(f32)
    bk = _bucket(np.maximum(jj - p, 0))
    c_Tm = np.ascontiguousarray(np.transpose(rel_bias[bk], (0, 2, 1)).reshape(128, 8 * 256)).astype(f32)
    c_b31 = np.ascontiguousarray(np.broadcast_to(rel_bias[31][None, :], (128, 8))).astype(f32)
    half = 16
    inv = (10000.0 ** (-np.arange(half, dtype=np.float32) / half)).astype(np.float32)
    ang = np.arange(SEQ, dtype=np.float32)[None, :] * inv[:, None]
    cosv, sinv = np.cos(ang).astype(f32), np.sin(ang).astype(f32)
    c_cos = np.concatenate([cosv, cosv], 0)
    c_sin = np.concatenate([-sinv, sinv], 0)
    return dict(wsrc=wsrc, c_gains=c_gains, c_tri=c_tri, c_E=c_E, c_caus=c_caus, c_Tm=c_Tm, c_b31=c_b31,
                c_cos=c_cos, c_sin=c_sin)


_CACHE = {}


def kernel(x, w_in, rel_bias, mla_q_norm, w_uq, mla_kv_norm, w_uk, w_uv, w_proj_a, w_proj_b,
           w_out, norm_attn, norm_mlp, w_mlp_up, w_mlp_down, norm_final):
    x = np.asarray(x, np.float32)
    args = [np.asarray(a, np.float32) for a in (w_in, rel_bias, mla_q_norm, w_uq, mla_kv_norm, w_uk, w_uv, w_proj_a,
                                                w_proj_b, w_out, norm_attn, norm_mlp, w_mlp_up, w_mlp_down, norm_final)]
    shared = _host_layout(*args)
    if "nc" not in _CACHE:
        seq = build(None)
        _CACHE["nc"] = build(seq)
    nc = _CACHE["nc"]
    in_maps = []
    for b in range(8):
        m = dict(shared)
        m["xT"] = np.ascontiguousarray(x[b].T)
        in_maps.append(m)
    res = run_bass_kernel_spmd(nc, in_maps, core_ids=list(range(8)))
    out = np.stack([np.ascontiguousarray(r["outT"].T) for r in res.results], axis=0)
    return out.astype(np.float32)
```

```python
import math
import numpy as np
import concourse.bass as bass
import concourse.mybir as mybir
from concourse.bass_utils import run_bass_kernel_spmd

F32 = mybir.dt.float32
BF16 = mybir.dt.bfloat16
ALU = mybir.AluOpType
AF = mybir.ActivationFunctionType
AX = mybir.AxisListType

SEQ = 4096
TC = 512
NCH = SEQ // TC
NEG = -30000.0
EPS = 1e-6
RING = 8
SL_Q, SL_K, SL_V, SL_CQ, SL_CKV, SL_KR, SL_UV, SL_HEAD = 0, 4, 8, 12, 15, 17, 19, 20
SL_G, SL_P, SL_O, SL_UP, SL_DN = 28, 44, 52, 60, 92
NSLAB = 124


class Task:
    __slots__ = ("eng", "fn", "deps", "sem", "val", "ndma", "has_waiter")

    def __init__(self, eng, fn, ndma):
        self.eng, self.fn, self.ndma = eng, fn, ndma
        self.deps, self.sem, self.val, self.has_waiter = [], None, None, False


class Sched:
    ENGS = ("pe", "act", "dve", "pool", "sp")

    def __init__(self):
        self.prog = {e: [] for e in self.ENGS}
        self.last_w, self.readers, self.chan_count, self.chan_last = {}, {}, {}, {}

    def add(self, eng, fn, reads=(), writes=(), chan=None, ndma=0, extra=()):
        t = Task(eng, fn, ndma)
        if chan is not None:
            c = self.chan_count.get(chan, 0) + 16 * ndma
            self.chan_count[chan] = c
            t.sem, t.val = ("chan", chan), c
            self.chan_last[chan] = t
        deps = {}
        for k in reads:
            w = self.last_w.get(k)
            if w is not None:
                deps[id(w)] = (w, "raw")
        for k in writes:
            w = self.last_w.get(k)
            if w is not None and id(w) not in deps:
                deps[id(w)] = (w, "waw")
            for r in self.readers.get(k, ()):
                if id(r) not in deps:
                    deps[id(r)] = (r, "war")
        for w in extra:
            deps[id(w)] = (w, "raw")
        for w, kind in deps.values():
            if w is t:
                continue
            if w.ndma == 0 and ndma == 0 and w.eng == eng and eng == "pe":
                continue
            t.deps.append(w)
            w.has_waiter = True
        for k in reads:
            self.readers.setdefault(k, []).append(t)
        for k in writes:
            self.last_w[k] = t
            self.readers[k] = []
        self.prog[eng].append(t)
        return t

    def barrier(self):
        first = [self.prog[e][-1] for e in self.ENGS if self.prog[e]]
        ex = first + list(self.chan_last.values())
        for e in self.ENGS:
            self.add(e, lambda h: h.nop(), extra=ex)

    def assign(self):
        for e in self.ENGS:
            cnt = 0
            for t in self.prog[e]:
                if t.ndma == 0 and t.has_waiter:
                    cnt += 1
                    t.sem, t.val = ("eng", e), cnt

    def semnames(self):
        s = set(("eng", e) for e in self.ENGS)
        for e in self.ENGS:
            for t in self.prog[e]:
                if t.sem is not None:
                    s.add(t.sem)
        return sorted(s)

    def emit_engine(self, e, h, sems):
        known = {}
        for t in self.prog[e]:
            need = {}
            for d in t.deps:
                if d.val > need.get(d.sem, 0):
                    need[d.sem] = d.val
            for s, v in need.items():
                if known.get(s, 0) >= v:
                    continue
                h.wait_ge(sems[s], v)
                known[s] = v
            r = t.fn(h)
            if t.ndma > 0:
                for ins in r:
                    ins.then_inc(sems[t.sem], 16)
            elif t.has_waiter:
                r.then_inc(sems[t.sem], 1)


DEBUG = False


def build(seq):
    nc = bass.Bass("TRN2", target_bir_lowering=False)
    din = lambda n, s: nc.dram_tensor(n, s, F32, kind="ExternalInput").ap()
    xT = din("xT", [1024, SEQ])
    wsrc = din("wsrc", [NSLAB, 128, 1024])
    c_gains = din("c_gains", [128, 32])
    c_tri = din("c_tri", [128, 128])
    c_E = din("c_E", [16, SEQ])
    c_caus = din("c_caus", [128, 256])
    c_Tm = din("c_Tm", [128, 8 * 256])
    c_b31 = din("c_b31", [128, 8])
    c_cos = din("c_cos", [32, SEQ])
    c_sin = din("c_sin", [32, SEQ])
    outT = nc.dram_tensor("outT", [1024, SEQ], F32, kind="ExternalOutput").ap()
    wscr = nc.dram_tensor("wscr", [NSLAB, 128, 1024], BF16, kind="Internal").ap()
    if DEBUG:
        dbg_oa = nc.dram_tensor("dbg_oa", [128, 4 * SEQ], BF16, kind="ExternalOutput").ap()
        dbg_ob = nc.dram_tensor("dbg_ob", [128, 4 * SEQ], BF16, kind="ExternalOutput").ap()
        dbg_h = nc.dram_tensor("dbg_h", [128, 8, SEQ], F32, kind="ExternalOutput").ap()
        dbg_h2 = nc.dram_tensor("dbg_h2", [128, 8, SEQ], F32, kind="ExternalOutput").ap()
        dbg_mg = nc.dram_tensor("dbg_mg", [128, 8, SEQ], BF16, kind="ExternalOutput").ap()
    xT3 = xT.rearrange("(k p) t -> p k t", p=128)
    outT3 = outT.rearrange("(k p) t -> p k t", p=128)

    S = Sched()
    rec = [] if seq is None else None

    SB_BASE = 16512 + 64
    top = [SB_BASE]
    cnt = [0]

    def sb(shape, dt):
        n = 1
        for s_ in shape[1:]:
            n *= s_
        nbytes = n * (4 if dt == F32 else 2)
        nbytes = (nbytes + 63) // 64 * 64
        off = top[0]
        top[0] += nbytes
        assert top[0] <= 229376, ("SBUF overflow", top[0])
        cnt[0] += 1
        return nc.alloc_sbuf_tensor_at(f"t{cnt[0]}", shape, dt, offset=off)

    psd = [nc.alloc_psum_tensor(f"psd{i}", [128, 1024], F32) for i in range(4)]
    ps = [psd[b // 2][:, (b % 2) * 512:(b % 2 + 1) * 512] for b in range(8)]
    psT = psd[3].bitcast(BF16)[:, 1024:2048]
    rr_b = [0]

    def psb():
        rr_b[0] ^= 1
        return 6 + rr_b[0]
    rr_g = [0]
    rr_m = [0]

    def psg():
        rr_g[0] = (rr_g[0] + 1) % 6
        return rr_g[0]

    def psm():
        return 6

    def T(eng, fn, r=(), w=(), **kw):
        return S.add(eng, fn, reads=r, writes=w, **kw)

    def mm(out, lhsT, rhs, r, w, start=True, stop=True):
        T("pe", lambda e: e.matmul(out, lhsT=lhsT, rhs=rhs, start=start, stop=stop), r, w)

    def actf(out, in_, func, r, w, scale=1.0, bias=None):
        if bias is None:
            T("act", lambda e: e.activation(out=out, in_=in_, func=func, scale=scale), r, w)
        else:
            T("act", lambda e: e.activation(out=out, in_=in_, func=func, scale=scale, bias=bias), r, w)

    def tt(eng, out, in0, in1, op, r, w):
        T(eng, lambda e: e.tensor_tensor(out=out, in0=in0, in1=in1, op=op), r, w)

    def ts(eng, out, in0, s1, op0, r, w, s2=None, op1=None):
        if op1 is None:
            T(eng, lambda e: e.tensor_scalar(out=out, in0=in0, scalar1=s1, scalar2=None, op0=op0), r, w)
        else:
            T(eng, lambda e: e.tensor_scalar(out=out, in0=in0, scalar1=s1, scalar2=s2, op0=op0, op1=op1), r, w)

    def stt(eng, out, in0, sc, in1, op0, op1, r, w):
        T(eng, lambda e: e.scalar_tensor_tensor(out=out, in0=in0, scalar=sc, in1=in1, op0=op0, op1=op1), r, w)

    def cp(eng, out, in_, r, w):
        if eng == "act":
            T("act", lambda e: e.activation(out=out, in_=in_, func=AF.Copy), r, w)
        else:
            T(eng, lambda e: e.tensor_copy(out=out, in_=in_), r, w)

    def mset(eng, ap, v, w):
        T(eng, lambda e: e.memset(ap, v), (), w)

    def dma(eng, out, in_, r, w, chan):
        T(eng, lambda e: [e.dma_start(out=out, in_=in_)], r, w, chan=chan, ndma=1)

    ring = [sb([128, 1024], BF16) for _ in range(RING)]
    oaT = sb([128, 4, SEQ], BF16)
    ident = sb([128, 128], BF16)
    ones = sb([128, 128], BF16)
    tri = sb([128, 128], BF16)
    gt = sb([128, 32], F32)
    gfin = gt[:, 21:29]
    att_base = top[0]
    Ptp = [sb([128, 2 * TC], BF16) for _ in range(4)]
    ofs = [sb([128, TC], F32) for _ in range(3)]
    DEF = []
    itc = [0]
    ofn = [0]

    def run_deferred(force=False):
        while DEF and (force or DEF[0][0] <= itc[0]):
            DEF.pop(0)[1]()
    rrfs = [sb([128, TC], BF16) for _ in range(3)]
    free_slots = [0, 1, 2]

    def slot():
        return free_slots.pop(0)

    def release(sl_):
        free_slots.append(sl_)
    stg = [sb([128, TC], BF16) for _ in range(2)]
    glob_top = top[0]

    class WM:
        pos = 0
        issued = 0

        def issue(self, i):
            slot = i % RING
            sl = seq[i]
            dma("sp", ring[slot][:], wscr[sl], [("scr", sl)], [("ring", slot)], f"ring{slot}")

        def get(self, sl, deep=False):
            i = self.pos
            self.pos += 1
            if seq is None:
                rec.append(sl)
                slot = i % RING
                dma("sp", ring[slot][:], wscr[sl], [("scr", sl)], [("ring", slot)], f"ring{slot}")
                return slot
            assert seq[i] == sl, (i, seq[i], sl)
            la = RING - 1 if deep else RING - 3
            while self.issued < min(len(seq), i + la):
                self.issue(self.issued)
                self.issued += 1
            return i % RING

    W = WM()

    mset("pool", ident[:], 1.0, ["ident"])
    T("pool", lambda e: e.affine_select(out=ident[:], in_=ident[:], pattern=[[-1, 128]], compare_op=ALU.is_equal,
                                        fill=0.0, base=0, channel_multiplier=1), ["ident"], ["ident"])
    mset("pool", ones[:], 1.0, ["ones"])
    dma("pool", tri[:], c_tri, [], ["tri"], "c0")
    dma("sp", gt[:], c_gains, [], ["gfin", "gains"], "c1")

    class WPrep:
        order = None
        pos = 0

        def step(self, n):
            for _ in range(n):
                if self.pos >= len(self.order):
                    return
                s_ = self.order[self.pos]
                ch = self.pos % 8
                self.pos += 1
                dma("pool", wscr[s_], wsrc[s_], [], [("scr", s_), ("wpch", ch)], f"wp{ch}")

    WP = WPrep()
    if seq is None:
        WP.order = list(range(NSLAB))
    else:
        a1s = [SL_Q, SL_Q + 1, SL_K, SL_K + 1, SL_V, SL_V + 1, SL_V + 2, SL_V + 3, SL_Q + 2, SL_Q + 3, SL_K + 2, SL_K + 3]
        seen, o = set(a1s), list(a1s)
        for s_ in seq:
            if s_ not in seen:
                seen.add(s_)
                o.append(s_)
        WP.order = o

    Bf = {}

    def alloc_chunk(nxn=1, nxs=1):
        Bf["xs"] = [sb([128, 8, TC], F32) for _ in range(nxs)]
        Bf["xq"] = sb([128, 8, TC], BF16)
        Bf["xn"] = [sb([128, 8, TC], BF16) for _ in range(nxn)]
        Bf["sd"] = sb([128, TC], F32)
        Bf["rstd"] = sb([128, TC], F32)

    def ln_dma(c):
        i = c % len(Bf["xs"])
        xs = Bf["xs"][i]
        cs = slice(c * TC, (c + 1) * TC)
        dma("sp", xs[:], xT3[:, :, cs], [], [("xs", i, k) for k in range(8)], f"xs{i}")

    def ln_sq(c):
        i = c % len(Bf["xs"])
        xs, xq = Bf["xs"][i], Bf["xq"]
        T("act", lambda e: e.activation(out=xq[:], in_=xs[:], func=AF.Square), [("xs", i, k) for k in range(8)], ["xq"])

    def ln_load(c):
        ln_dma(c)
        ln_sq(c)

    def ln_ss(c):
        xq, sd, rstd = Bf["xq"], Bf["sd"], Bf["rstd"]
        pb = psm()
        for kt in range(8):
            mm(ps[pb][:], ones[:], xq[:, kt, :], ["xq", "ones"], [("ps", pb)], start=(kt == 0), stop=(kt == 7))
        actf(sd[:], ps[pb][:], AF.Ln, [("ps", pb)], ["sd"], scale=1.0 / 1024, bias=EPS)
        actf(rstd[:], sd[:], AF.Exp, ["sd"], ["rstd"], scale=-0.5)

    def ln_xn(c, j):
        i = c % len(Bf["xs"])
        xs, xn, rstd = Bf["xs"][i], Bf["xn"], Bf["rstd"]
        for kt in range(8):
            stt("dve", xn[j][:, kt, :], xs[:, kt, :], gt[:, kt:kt + 1], rstd[:], ALU.mult, ALU.mult,
                [("xs", i, kt), "rstd", "gains"], [("xn", j, kt)])

    def load_norm(c, j):
        ln_load(c)
        ln_ss(c)
        ln_xn(c, j)

    def rms_fm(src, nt, nfeat, dstf, key_src, key_dst, g0):
        xq, sd, rstd = Bf["xq"], Bf["sd"], Bf["rstd"]
        T("act", lambda e: e.activation(out=xq[:, 0:nt, :], in_=src[:, 0:nt, :], func=AF.Square), [key_src(k) for k in range(nt)], ["xq"])
        pb = psm()
        for kt in range(nt):
            mm(ps[pb][:], ones[:], xq[:, kt, :], ["xq", "ones"], [("ps", pb)], start=(kt == 0), stop=(kt == nt - 1))
        actf(sd[:], ps[pb][:], AF.Ln, [("ps", pb)], ["sd"], scale=1.0 / nfeat, bias=EPS)
        actf(rstd[:], sd[:], AF.Exp, ["sd"], ["rstd"], scale=-0.5)
        for kt in range(nt):
            stt("dve", dstf(kt), src[:, kt, :], gt[:, g0 + kt:g0 + kt + 1], rstd[:], ALU.mult, ALU.mult, [key_src(kt), "rstd", "gains"], [key_dst(kt)])

    o_banks = [6, 7]

    def attention(tiles, lhsT_of, rhs_of, vx_of, exp_scale, kkeys, qkeys, vkeys, finish, side=()):
        flat = []
        for c in range(NCH):
            tl = tiles(c)
            for i, (kt, lo, masks) in enumerate(tl):
                flat.append((c, kt, lo, masks, i == 0, i == len(tl) - 1))
        pairs = [flat[i:i + 2] for i in range(0, len(flat), 2)]
        side = list(side)
        per = (len(side) + len(pairs) - 1) // max(1, len(pairs))
        slots = {}

        def qk(j):
            pd = slot()
            slots[j] = pd
            for u, (c, kt, lo, masks, first, last) in enumerate(pairs[j]):
                o = 512 * u
                mm(psd[pd][:, o + lo:o + TC], lhsT_of(kt), rhs_of(c, lo), kkeys + qkeys(c), [("psd", pd, u)], start=True, stop=(len(masks) == 0))
                for mi, (c0, n, mAP, mk) in enumerate(masks):
                    mm(psd[pd][:, o + c0:o + c0 + n], ident[:], mAP, ["ident"] + mk, [("psd", pd, u)], start=False, stop=(mi == len(masks) - 1))

        qk(0)
        if len(pairs) > 1:
            qk(1)
        for j in range(len(pairs)):
            if j + 2 < len(pairs):
                qk(j + 2)
            pd, pt = slots[j], j % 4
            pr = pairs[j]
            if len(pr) == 2 and pr[1][2] == 0:
                lo0 = pr[0][2]
                actf(Ptp[pt][:, lo0:2 * TC], psd[pd][:, lo0:2 * TC], AF.Exp, [("psd", pd, 0), ("psd", pd, 1)], [("Pt", pt, 0), ("Pt", pt, 1)], scale=exp_scale)
            else:
                for u, (c, kt, lo, masks, first, last) in enumerate(pr):
                    o = 512 * u
                    actf(Ptp[pt][:, o + lo:o + TC], psd[pd][:, o + lo:o + TC], AF.Exp, [("psd", pd, u)], [("Pt", pt, u)], scale=exp_scale)
            release(pd)
            for u, (c, kt, lo, masks, first, last) in enumerate(pr):
                o = 512 * u
                ob = o_banks[c % 2]
                mm(ps[ob][0:96, lo:TC], vx_of(kt), Ptp[pt][:, o + lo:o + TC], [("Pt", pt, u)] + vkeys(kt), [("ps", ob)], start=first, stop=last)
                if last:
                    finish(c, ob)
            for _ in range(per):
                if side:
                    side.pop(0)()
            itc[0] += 1
            run_deferred()
        while side:
            side.pop(0)()

    def make_finish(dstT, h):
        f, hh = h // 2, h % 2

        def finish(c, ob):
            cs = slice(c * TC, (c + 1) * TC)
            oi = ofn[0] % 3
            ofn[0] += 1
            of = ofs[oi]
            ok = ("ofs", oi)
            cp("dve", of[0:65, :], ps[ob][0:65, :], [("ps", ob)], [ok])
            rrf = rrfs[oi]
            rk = ("rrf", oi)
            T("dve", lambda e: e.reciprocal(out=rrf[64:65, :], in_=of[64:65, :]), [ok], [rk])

            def stage_b():
                pd = slot()
                mm(psd[pd][0:64, 0:TC], ones[64:65, 0:64], rrf[64:65, :], [rk, "ones"], [("psd", pd, 0)])
                if hh == 0:
                    tt("dve", dstT[0:64, f, cs], of[0:64, :], psd[pd][0:64, 0:TC], ALU.mult, [ok, ("psd", pd, 0)], [("oT", id(dstT), f, c, 0)])
                else:
                    sg = stg[c % 2]
                    tt("dve", sg[0:64, :], of[0:64, :], psd[pd][0:64, 0:TC], ALU.mult, [ok, ("psd", pd, 0)], [("stg", c % 2)])
                    dma("sp", dstT[64:128, f, cs], sg[0:64, :], [("stg", c % 2)], [("oT", id(dstT), f, c, 1)], f"stg{c % 2}")
                release(pd)
            DEF.append((itc[0] + 7, stage_b))
        return finish

    oT_keys = lambda dstT, c: [("oT", id(dstT), f, c, hh) for f in range(4) for hh in range(2)]

    def bc16(ap2, n):
        return ap2.unsqueeze(2).to_broadcast([128, 16, n])

    for g in range(2):
        top[0] = glob_top
        qT = sb([128, 2, SEQ], BF16)
        kT = sb([128, 2, SEQ], BF16)
        Vx = sb([128, 32, 4, 96], BF16)
        ksum = sb([128, 2, 16], F32)
        KM = sb([128, 2, 32], BF16)
        M16 = sb([128, 16, 16], F32)
        Mb = [sb([128, 256], BF16) for _ in range(2)]
        gs = sb([128, 16, 16], F32)
        g1 = sb([128, 16, 16], F32)
        eq = sb([128, 16, 16], F32)
        mx = sb([128, 16], F32)
        MT = sb([128, SEQ], BF16)
        a1_top = top[0]
        wres = [sb([128, 1024], BF16) for _ in range(4)]
        alloc_chunk(2, 2)
        xn = Bf["xn"]
        mset("pool", Vx[:, :, :, 64:96], 1.0, [("Vx", t_) for t_ in range(32)])
        mset("pool", KM[:], 0.0, ["KM"])
        if g == 0:
            WP.step(8)
        for i_, sl_ in enumerate([SL_Q + 2 * g, SL_Q + 2 * g + 1, SL_K + 2 * g, SL_K + 2 * g + 1]):
            dma("sp", wres[i_][:], wscr[sl_], [("scr", sl_)], [("wres", i_)], f"wres{i_}")

        def gate_part1(c, qT=qT, KM=KM, M16=M16, Mb=Mb, gs=gs, g1=g1, eq=eq, mx=mx, ksum=ksum):
            for fl in range(2):
                ts("dve", KM[0:64, fl, 2 * c:2 * c + 2], ksum[0:64, fl, 2 * c:2 * c + 2], 1.0 / 256, ALU.mult, ["ksum"], ["KM"])
                ts("dve", KM[64:128, fl, 16 + 2 * c:18 + 2 * c], ksum[64:128, fl, 2 * c:2 * c + 2], 1.0 / 256, ALU.mult, ["ksum"], ["KM"])
            qlo, qhi = 2 * c, 2 * c + 1
            mset("pool", M16[:], NEG, ["M16"])
            if qhi <= 3:
                mset("pool", M16[:, 0:8, 0:qlo + 1], 0.0, ["M16"])
                mset("pool", M16[:, 8:16, 0:qhi + 1], 0.0, ["M16"])
            else:
                pb = psm()
                for t4 in range(4):
                    t = 4 * c + t4
                    for fl in range(2):
                        mm(ps[pb][:, t4 * 64 + fl * 32:t4 * 64 + (fl + 1) * 32], qT[:, fl, t * 128:(t + 1) * 128], KM[:, fl, :],
                           [("qT", fl, c), "KM"], [("ps", pb)])
                g3 = ps[pb][:, 0:256].rearrange("p (g n) -> p g n", g=16)[:, :, 0:qhi]
                cp("dve", gs[:, :, 0:qhi], g3, [("ps", pb)], ["gs"])
                T("dve", lambda e: e.memset(gs[:, 0:8, qlo:qlo + 1], -1.0e9), (), ["gs"])
                src = gs
                for rnd in range(3):
                    T("dve", lambda e, src=src: e.reduce_max(out=mx[:], in_=src[:, :, 0:qhi], axis=AX.X), ["gs", "g1"], ["mx"])
                    if rnd < 2:
                        tt("dve", eq[:, :, 0:qhi], src[:, :, 0:qhi], bc16(mx[:], qhi), ALU.is_ge, ["gs", "g1", "mx"], ["eq"])
                        stt("dve", g1[:, :, 0:qhi], eq[:, :, 0:qhi], -1.0e9, src[:, :, 0:qhi], ALU.mult, ALU.add, ["eq", "gs", "g1"], ["g1"])
                        src = g1
                tt("dve", eq[:, :, 0:qhi], gs[:, :, 0:qhi], bc16(mx[:], qhi), ALU.is_ge, ["gs", "mx"], ["eq"])
                ts("dve", M16[:, :, 0:qhi], eq[:, :, 0:qhi], -NEG, ALU.mult, ["eq"], ["M16"], s2=NEG, op1=ALU.add)
                T("dve", lambda e: e.memset(M16[:, 0:8, qlo:qlo + 1], 0.0), (), ["M16"])
                T("dve", lambda e: e.memset(M16[:, 8:16, qhi:qhi + 1], 0.0), (), ["M16"])
            cp("pool", Mb[c % 2][:], M16[:].rearrange("p g n -> p (g n)"), ["M16"], [("Mb", c % 2)])

        def gate_part2(c, Mb=Mb, MT=MT):
            for t4 in range(4):
                T("pe", lambda e, t4=t4, Mb=Mb: e.transpose(psT[0:64, t4 * 128:(t4 + 1) * 128], Mb[c % 2][:, t4 * 64:(t4 + 1) * 64], ident[:]),
                  [("Mb", c % 2), "ident"], [("ps", 7)])
            cp("act", MT[0:64, c * TC:(c + 1) * TC], psT[0:64, 0:TC], [("ps", 7)], [("MT", c)])

        ln_load(0)
        ln_ss(0)
        ln_xn(0, 0)
        ln_dma(1)
        for c in range(NCH):
            j = c % 2
            cs = slice(c * TC, (c + 1) * TC)
            if c + 2 < NCH:
                ln_dma(c + 2)
            if g == 0:
                WP.step(1)
            for fl in range(2):
                pb = psg()
                for kt in range(8):
                    mm(ps[pb][:], wres[fl][:, kt * 128:(kt + 1) * 128], xn[j][:, kt, :], [("wres", fl), ("xn", j, kt)], [("ps", pb)],
                       start=(kt == 0), stop=(kt == 7))
                actf(qT[:, fl, cs], ps[pb][:], AF.Identity, [("ps", pb)], [("qT", fl, c)], scale=0.125)
            if c + 1 < NCH:
                ln_sq(c + 1)
            if c > 0:
                gate_part2(c - 1)
            for fl in range(2):
                pb = psg()
                for kt in range(8):
                    mm(ps[pb][:], wres[2 + fl][:, kt * 128:(kt + 1) * 128], xn[j][:, kt, :], [("wres", 2 + fl), ("xn", j, kt)], [("ps", pb)],
                       start=(kt == 0), stop=(kt == 7))
                cp("act", kT[:, fl, cs], ps[pb][:], [("ps", pb)], [("kT", fl, c)])
                T("dve", lambda e, fl=fl, c=c, cs=cs, kT=kT, ksum=ksum: e.reduce_sum(
                    out=ksum[:, fl, 2 * c:2 * c + 2], in_=kT[:, fl, cs].rearrange("p (b t) -> p b t", b=2), axis=AX.X),
                  [("kT", fl, c)], ["ksum"])
            if c + 1 < NCH:
                ln_ss(c + 1)
            pbs = [psg() for _ in range(4)]
            for jv in range(4):
                sl = W.get(SL_V + jv)
                for t4 in range(4):
                    for kk in range(2):
                        kt = 2 * jv + kk
                        mm(ps[pbs[t4]][:, 0:256], xn[j][:, kt, t4 * 128:(t4 + 1) * 128],
                           ring[sl][:, kk * 512 + g * 256:kk * 512 + (g + 1) * 256],
                           [("ring", sl), ("xn", j, kt)], [("ps", pbs[t4])], start=(kt == 0), stop=(kt == 7))
            if c + 1 < NCH:
                ln_xn(c + 1, (c + 1) % 2)
            for t4 in range(4):
                cp("act", Vx[:, 4 * c + t4, :, 0:64], ps[pbs[t4]][:, 0:256].rearrange("p (h d) -> p h d", h=4),
                   [("ps", pbs[t4])], [("Vx", 4 * c + t4)])
            gate_part1(c)
        gate_part2(NCH - 1)
        S.barrier()

        top[0] = a1_top
        Qaug = [sb([128, SEQ], BF16) for _ in range(2)]
        Kaug = [sb([128, SEQ], BF16) for _ in range(2)]
        Dtmp = sb([128, 8 * 256], F32)
        caus = sb([128, 256], F32)
        b31 = sb([128, 8], F32)
        Dm = sb([128, 8, 256], BF16)
        dma("sp", Dtmp[:], c_Tm, [], ["Dtmp"], "c3")
        dma("sp", caus[:], c_caus, [], ["caus"], "c4")
        dma("sp", b31[:], c_b31, [], ["b31"], "c5")
        for h in range(8):
            stt("dve", Dm[:, h, :], Dtmp[:, h * 256:(h + 1) * 256], b31[:, h:h + 1], caus[:], ALU.subtract, ALU.add,
                ["Dtmp", "b31", "caus"], [("Dm", h)])
        for b in range(2):
            mset("pool", Qaug[b][64:128, :], 0.0, [("Qaug", b, "m")])
            mset("pool", Kaug[b][64:128, :], 0.0, [("Kaug", b, "e")])
            dma("pool", Kaug[b][64:80, :], c_E, [], [("Kaug", b, "e")], f"ce{b}")

        def moba_tiles(h):
            def tiles(c):
                out = []
                for kt in range(4 * c + 4):
                    jj = kt - 4 * c
                    lo = max(0, 128 * jj)
                    masks = []
                    if jj == -1:
                        masks = [(0, 128, Dm[:, h, 128:256], [("Dm", h)])]
                    elif jj >= 0:
                        n = min(256, TC - 128 * jj)
                        masks = [(128 * jj, n, Dm[:, h, 0:n], [("Dm", h)])]
                    out.append((kt, lo, masks))
                return out
            return tiles

        def head_copies(hl):
            b, fl, hh = hl % 2, hl // 2, hl % 2
            dma("sp", Qaug[b][0:64, :], qT[64 * hh:64 * hh + 64, fl, :], [("qT", fl, c) for c in range(NCH)], [("Qaug", b, "q")], f"qa{b}")
            dma("sp", Kaug[b][0:64, :], kT[64 * hh:64 * hh + 64, fl, :], [("kT", fl, c) for c in range(NCH)], [("Kaug", b, "k")], f"ka{b}")
            dma("sp", Qaug[b][64:80, :], MT[16 * hl:16 * hl + 16, :], [("MT", i) for i in range(8)], [("Qaug", b, "m")], f"qm{b}")

        head_copies(0)
        for hl in range(4):
            h = 4 * g + hl
            b = hl % 2
            side = []
            if hl + 1 < 4:
                side.append(lambda hl=hl: head_copies(hl + 1))
            side += [(lambda: WP.step(1)) for _ in range(40)]
            attention(moba_tiles(h),
                      lambda kt, b=b, Kaug=Kaug: Kaug[b][:, kt * 128:(kt + 1) * 128],
                      lambda c, lo, b=b, Qaug=Qaug: Qaug[b][:, c * TC + lo:(c + 1) * TC],
                      lambda kt, hl=hl, Vx=Vx: Vx[:, kt, hl, :],
                      1.0,
                      [("Kaug", b, "k"), ("Kaug", b, "e")],
                      lambda c, b=b: [("Qaug", b, "q"), ("Qaug", b, "m")],
                      lambda kt: [("Vx", kt)],
                      make_finish(oaT, h), side)
        run_deferred(True)
        S.barrier()

    top[0] = glob_top
    obT = sb([128, 4, SEQ], BF16)
    cqnT = sb([128, 3, SEQ], BF16)
    ckvnT = sb([128, 2, SEQ], BF16)
    krT = sb([128, SEQ], BF16)
    mla_top = top[0]
    alloc_chunk()
    xn = Bf["xn"]
    cqf = sb([128, 3, TC], F32)
    ckvf = sb([128, 2, TC], F32)
    cst = [sb([128, TC], F32) for _ in range(1)] * 2
    snt = [sb([128, TC], F32) for _ in range(1)] * 2
    tabmod = [1]
    t1 = sb([128, TC], F32)
    t2 = sb([128, TC], F32)
    a2_top = top[0]
    top[0] = mla_top
    cst2 = [sb([128, TC], F32) for _ in range(2)]
    snt2 = [sb([128, TC], F32) for _ in range(2)]
    t1b = sb([128, TC], F32)
    t2b = sb([128, TC], F32)
    KT = [sb([128, SEQ], BF16) for _ in range(2)]
    QTh = [sb([128, SEQ], BF16) for _ in range(2)]
    Vh = [sb([128, 32, 96], BF16) for _ in range(2)]

    tabn = [0]

    def load_tables(c):
        i = tabn[0] % tabmod[0]
        tabn[0] += 1
        cs = slice(c * TC, (c + 1) * TC)
        dma("sp", cst[i][64:96, :], c_cos[:, cs], [], [("cst", i)], f"cst{i}")
        dma("sp", snt[i][64:96, :], c_sin[:, cs], [], [("snt", i)], f"snt{i}")
        return i

    def rope(dst, p1, p2, ti, rkeys, wkeys):
        tt("dve", t1[64:96, :], p1, cst[ti][64:96, :], ALU.mult, rkeys[0:1] + [("cst", ti)], ["t1"])
        tt("dve", t2[64:96, :], p2, snt[ti][64:96, :], ALU.mult, rkeys[1:2] + [("snt", ti)], ["t2"])
        tt("pool", dst, t1[64:96, :], t2[64:96, :], ALU.add, ["t1", "t2"], wkeys)

    ln_load(0)
    ln_ss(0)
    ln_xn(0, 0)
    for c in range(NCH):
        j = 0
        cs = slice(c * TC, (c + 1) * TC)
        if c + 1 < NCH:
            ln_load(c + 1)
        ti = load_tables(c)
        for f in range(3):
            sl = W.get(SL_CQ + f)
            pb = psg()
            for kt in range(8):
                mm(ps[pb][:], ring[sl][:, kt * 128:(kt + 1) * 128], xn[j][:, kt, :], [("ring", sl), ("xn", j, kt)], [("ps", pb)],
                   start=(kt == 0), stop=(kt == 7))
            cp("act", cqf[:, f, :], ps[pb][:], [("ps", pb)], ["cqf"])
        for f in range(2):
            sl = W.get(SL_CKV + f)
            pb = psg()
            for kt in range(8):
                mm(ps[pb][:], ring[sl][:, kt * 128:(kt + 1) * 128], xn[j][:, kt, :], [("ring", sl), ("xn", j, kt)], [("ps", pb)],
                   start=(kt == 0), stop=(kt == 7))
            cp("dve", ckvf[:, f, :], ps[pb][:], [("ps", pb)], ["ckvf"])
        pk = []
        for i in range(2):
            sl = W.get(SL_KR + i)
            pb = psg()
            pk.append(pb)
            for kt in range(8):
                mm(ps[pb][0:96, :], ring[sl][:, kt * 128:kt * 128 + 96], xn[j][:, kt, :], [("ring", sl), ("xn", j, kt)], [("ps", pb)],
                   start=(kt == 0), stop=(kt == 7))
        rope(krT[64:96, cs], ps[pk[0]][64:96, :], ps[pk[1]][64:96, :], ti, [("ps", pk[0]), ("ps", pk[1])], [("krT", c)])
        rms_fm(cqf, 3, 384.0, lambda kt, cs=cs: cqnT[:, kt, cs], lambda k: "cqf", lambda kt, c=c: ("cqnT", c), 16)
        if c + 1 < NCH:
            ln_ss(c + 1)
            ln_xn(c + 1, 0)
        rms_fm(ckvf, 2, 256.0, lambda kt, cs=cs: ckvnT[:, kt, cs], lambda k: "ckvf", lambda kt, c=c: ("ckvnT", c), 19)
    S.barrier()
    cst[0], cst[1], snt[0], snt[1] = cst2[0], cst2[1], snt2[0], snt2[1]
    tabmod[0] = 2
    t1, t2 = t1b, t2b

    WP.step(NSLAB)
    for b in range(2):
        mset("pool", KT[b][96:128, :], 0.0, [("KT", b, "z")])
        mset("pool", QTh[b][96:128, :], 0.0, [("QT", b, "z")])
        mset("pool", Vh[b][:, :, 64:96], 1.0, [("Vh", b, g_) for g_ in range(4)])

    def mla_tiles(c):
        out = []
        for kt in range(4 * c + 4):
            jj = kt - 4 * c
            lo = max(0, 128 * jj)
            masks = [(128 * jj, 128, tri[:], ["tri"])] if jj >= 0 else []
            out.append((kt, lo, masks))
        return out

    def rope_dve(dst, p1, p2, ti, rkeys, wkeys):
        tt("dve", t1[64:96, :], p1, cst[ti][64:96, :], ALU.mult, rkeys[0:1] + [("cst", ti)], ["t1"])
        tt("dve", t2[64:96, :], p2, snt[ti][64:96, :], ALU.mult, rkeys[1:2] + [("snt", ti)], ["t2"])
        tt("dve", dst, t1[64:96, :], t2[64:96, :], ALU.add, ["t1", "t2"], wkeys)

    def mla_build(h):
        b = h % 2
        st = {}
        out = []

        def get_slabs():
            st["sl"] = W.get(SL_HEAD + h)
            st["slv"] = W.get(SL_UV)
        out.append(get_slabs)

        def kbuild(c):
            sl = st["sl"]
            cs = slice(c * TC, (c + 1) * TC)
            pd = slot()
            for kt in range(2):
                mm(psd[pd][0:64, 0:TC], ring[sl][:, (6 + kt) * 128:(6 + kt) * 128 + 64], ckvnT[:, kt, cs], [("ring", sl), ("ckvnT", c)], [("psd", pd, 0)],
                   start=(kt == 0), stop=(kt == 1))
            cp("act", KT[b][0:64, cs], psd[pd][0:64, 0:TC], [("psd", pd, 0)], [("KT", b, "n", c)])
            release(pd)
            cp("pool", KT[b][64:96, cs], krT[64:96, cs], [("krT", c)], [("KT", b, "r", c)])

        def vbuild(g_):
            slv = st["slv"]
            pd = slot()
            for i in range(8):
                t = 8 * g_ + i
                for kt in range(2):
                    mm(psd[pd][:, i * 64:(i + 1) * 64], ckvnT[:, kt, t * 128:(t + 1) * 128], ring[slv][:, kt * 512 + h * 64:kt * 512 + h * 64 + 64],
                       [("ring", slv), ("ckvnT", t // 4)], [("psd", pd, 0)], start=(kt == 0), stop=(kt == 1))
            cp("act", Vh[b][:, 8 * g_:8 * g_ + 8, 0:64], psd[pd][:, 0:TC].rearrange("p (i d) -> p i d", i=8), [("psd", pd, 0)], [("Vh", b, g_)])
            release(pd)

        def qbuild(c):
            sl = st["sl"]
            cs = slice(c * TC, (c + 1) * TC)
            ti = load_tables(c)
            pd = slot()
            for kt in range(3):
                mm(psd[pd][0:96, 0:TC], ring[sl][:, kt * 128:kt * 128 + 96], cqnT[:, kt, cs], [("ring", sl), ("cqnT", c)], [("psd", pd, 0)],
                   start=(kt == 0), stop=(kt == 2))
            for kt in range(3):
                mm(psd[pd][0:96, TC:2 * TC], ring[sl][:, (3 + kt) * 128:(3 + kt) * 128 + 96], cqnT[:, kt, cs], [("ring", sl), ("cqnT", c)], [("psd", pd, 1)],
                   start=(kt == 0), stop=(kt == 2))
            cp("act", QTh[b][0:64, cs], psd[pd][0:64, 0:TC], [("psd", pd, 0)], [("QT", b, "n", c)])
            rope_dve(QTh[b][64:96, cs], psd[pd][64:96, 0:TC], psd[pd][64:96, TC:2 * TC], ti, [("psd", pd, 0), ("psd", pd, 1)], [("QT", b, "r", c)])
            release(pd)

        for c in range(NCH):
            out.append(lambda c=c: kbuild(c))
        for g_ in range(4):
            out.append(lambda g_=g_: vbuild(g_))
        for c in range(NCH):
            out.append(lambda c=c: qbuild(c))
        return out

    for fn in mla_build(0):
        fn()
    for h in range(8):
        b = h % 2
        side = []
        if h + 1 < 8:
            for fn in mla_build(h + 1):
                fn()
        attention(mla_tiles,
                  lambda kt, b=b: KT[b][:, kt * 128:(kt + 1) * 128],
                  lambda c, lo, b=b: QTh[b][:, c * TC + lo:(c + 1) * TC],
                  lambda kt, b=b: Vh[b][:, kt, :],
                  96.0 ** -0.5,
                  [("KT", b, "z")] + [("KT", b, "n", c) for c in range(NCH)] + [("KT", b, "r", c) for c in range(NCH)],
                  lambda c, b=b: [("QT", b, "n", c), ("QT", b, "r", c), ("QT", b, "z")],
                  lambda kt, b=b: [("Vh", b, kt // 8)],
                  make_finish(obT, h), side)
    run_deferred(True)
    S.barrier()
    if DEBUG:
        dma("sp", dbg_oa, oaT[:].rearrange("p a t -> p (a t)"), [], [], "dbg")
        dma("sp", dbg_ob, obT[:].rearrange("p a t -> p (a t)"), [], [], "dbg")
        S.barrier()
    top[0] = glob_top
    obT_c = sb([128, 4, SEQ], BF16)
    xs2 = [sb([128, 8, TC], F32) for _ in range(2)]
    xq = sb([128, 8, TC], BF16)
    xnc = sb([128, 8, TC], BF16)
    sd = sb([128, TC], F32)
    rstd = sb([128, TC], F32)
    r2 = sb([128, TC], F32)
    mg = sb([128, 8, TC], BF16)
    mnb = sb([128, 8, TC], BF16)
    upT = sb([128, 32, TC], BF16)
    _cur = top[0]
    top[0] = att_base
    sga = sb([128, TC], F32)
    sgb = sb([128, TC], F32)
    ta = sb([128, TC], F32)
    tb_ = sb([128, TC], F32)
    rl = [sb([128, TC], F32) for _ in range(2)]
    assert top[0] <= glob_top
    top[0] = _cur

    def xk(i):
        return [("xs", i, k) for k in range(8)]

    def c_load(c):
        i = c % 2
        dma("sp", xs2[i][:], xT3[:, :, c * TC:(c + 1) * TC], [], xk(i), f"xs{i}")

    def c_norm_a(c):
        i = c % 2
        T("act", lambda e: e.activation(out=xq[:], in_=xs2[i][:], func=AF.Square), xk(i), ["xq"])

    def c_norm_b(c):
        i = c % 2
        for kt in range(8):
            mm(ps[6][:], ones[:], xq[:, kt, :], ["xq", "ones"], [("ps", 6)], start=(kt == 0), stop=(kt == 7))
        actf(sd[:], ps[6][:], AF.Ln, [("ps", 6)], ["sd"], scale=1.0 / 1024, bias=EPS)
        actf(rstd[:], sd[:], AF.Exp, ["sd"], ["rstd"], scale=-0.5)
        for kt in range(8):
            stt("dve", xnc[:, kt, :], xs2[i][:, kt, :], gt[:, kt:kt + 1], rstd[:], ALU.mult, ALU.mult,
                [("xs", i, kt), "rstd", "gains"], [("xn", kt)])

    c_load(0)
    c_norm_a(0)
    c_norm_b(0)
    for c in range(NCH):
        cs = slice(c * TC, (c + 1) * TC)
        i = c % 2
        xs = xs2[i]
        if c + 1 < NCH:
            c_load(c + 1)
        for f in range(8):
            sa, sbb, sp_ = W.get(SL_G + f), W.get(SL_G + 8 + f), W.get(SL_P + f)
            pga, pgb, ppa, ppb = psg(), psg(), psg(), psg()
            for kt in range(8):
                mm(ps[pga][:], ring[sa][:, kt * 128:(kt + 1) * 128], xnc[:, kt, :], [("ring", sa), ("xn", kt)], [("ps", pga)],
                   start=(kt == 0), stop=(kt == 7))
            for kt in range(8):
                mm(ps[pgb][:], ring[sbb][:, kt * 128:(kt + 1) * 128], xnc[:, kt, :], [("ring", sbb), ("xn", kt)], [("ps", pgb)],
                   start=(kt == 0), stop=(kt == 7))
            for kt in range(4):
                mm(ps[ppa][:], ring[sp_][:, kt * 128:(kt + 1) * 128], oaT[:, kt, cs], [("ring", sp_)] + oT_keys(oaT, c), [("ps", ppa)],
                   start=(kt == 0), stop=(kt == 3))
            for kt in range(4):
                mm(ps[ppb][:], ring[sp_][:, (4 + kt) * 128:(5 + kt) * 128], obT[:, kt, cs], [("ring", sp_)] + oT_keys(obT, c), [("ps", ppb)],
                   start=(kt == 0), stop=(kt == 3))
            actf(sga[:], ps[pga][:], AF.Sigmoid, [("ps", pga)], ["sga"])
            actf(sgb[:], ps[pgb][:], AF.Sigmoid, [("ps", pgb)], ["sgb"])
            tt("dve", ta[:], sga[:], ps[ppa][:], ALU.mult, ["sga", ("ps", ppa)], ["ta"])
            tt("dve", tb_[:], sgb[:], ps[ppb][:], ALU.mult, ["sgb", ("ps", ppb)], ["tb"])
            tt("pool", mg[:, f, :], ta[:], tb_[:], ALU.add, ["ta", "tb"], [("mg", f)])
        for f in range(8):
            sl = W.get(SL_O + f, deep=True)
            pb = psg()
            for kt in range(8):
                mm(ps[pb][:], ring[sl][:, kt * 128:(kt + 1) * 128], mg[:, kt, :], [("ring", sl), ("mg", kt)], [("ps", pb)],
                   start=(kt == 0), stop=(kt == 7))
            tt("dve", xs[:, f, :], xs[:, f, :], ps[pb][:], ALU.add, [("xs", i, f), ("ps", pb)], [("xs", i, f)])
            ts("dve", mnb[:, f, :], xs[:, f, :], gt[:, 8 + f:9 + f], ALU.mult, [("xs", i, f), "gains"], [("mnb", f)])
        if DEBUG:
            dma("sp", dbg_h[:, :, cs], xs[:], xk(i), [], "dbg")
            dma("sp", dbg_mg[:, :, cs], mg[:], [("mg", k) for k in range(8)], [], "dbg")
        T("act", lambda e, xs=xs: e.activation(out=xq[:], in_=xs[:], func=AF.Square), xk(i), ["xq"])
        for kt in range(8):
            mm(ps[6][:], ones[:], xq[:, kt, :], ["xq", "ones"], [("ps", 6)], start=(kt == 0), stop=(kt == 7))
        actf(sd[:], ps[6][:], AF.Ln, [("ps", 6)], ["sd"], scale=1.0 / 1024, bias=EPS)
        actf(r2[:], sd[:], AF.Exp, ["sd"], ["r2"], scale=-1.0)
        for ff in range(32):
            sl = W.get(SL_UP + ff, deep=True)
            pb = psg()
            for kt in range(8):
                mm(ps[pb][:], ring[sl][:, kt * 128:(kt + 1) * 128], mnb[:, kt, :], [("ring", sl), ("mnb", kt)], [("ps", pb)],
                   start=(kt == 0), stop=(kt == 7))
            actf(rl[ff % 2][:], ps[pb][:], AF.Relu, [("ps", pb)], [("rl", ff % 2)])
            tt("pool", upT[:, ff, :], rl[ff % 2][:], rl[ff % 2][:], ALU.mult, [("rl", ff % 2)], [("upT", ff)])
            if c + 1 < NCH:
                if ff == 4:
                    c_norm_a(c + 1)
                if ff == 12:
                    c_norm_b(c + 1)
        for f in range(8):
            pb = psg()
            for jd in range(4):
                sl = W.get(SL_DN + f * 4 + jd, deep=True)
                for i8 in range(8):
                    ff = 8 * jd + i8
                    mm(ps[pb][:], ring[sl][:, i8 * 128:(i8 + 1) * 128], upT[:, ff, :], [("ring", sl), ("upT", ff)], [("ps", pb)],
                       start=(ff == 0), stop=(ff == 31))
            tt("dve", ta[:], ps[pb][:], r2[:], ALU.mult, [("ps", pb), "r2"], ["ta"])
            tt("pool", xs[:, f, :], xs[:, f, :], ta[:], ALU.add, [("xs", i, f), "ta"], [("xs", i, f)])
        if DEBUG:
            dma("sp", dbg_h2[:, :, cs], xs[:], xk(i), [], "dbg")
        T("act", lambda e, xs=xs: e.activation(out=xq[:], in_=xs[:], func=AF.Square), xk(i), ["xq"])
        for kt in range(8):
            mm(ps[7][:], ones[:], xq[:, kt, :], ["xq", "ones"], [("ps", 7)], start=(kt == 0), stop=(kt == 7))
        actf(sd[:], ps[7][:], AF.Ln, [("ps", 7)], ["sd"], scale=1.0 / 1024, bias=EPS)
        actf(rstd[:], sd[:], AF.Exp, ["sd"], ["rstd"], scale=-0.5)
        for f in range(8):
            stt("dve", xs[:, f, :], xs[:, f, :], gt[:, 21 + f:22 + f], rstd[:], ALU.mult, ALU.mult,
                [("xs", i, f), "gfin", "rstd"], [("xs", i, f)])
        dma("pool", outT3[:, :, cs], xs[:], xk(i), [("out", c)], f"out{i}")

    if seq is None:
        return rec

    S.assign()
    sems = {s_: nc.alloc_semaphore("s_" + "_".join(map(str, s_))) for s_ in S.semnames()}
    final_out = [(("chan", f"out{i}"), S.chan_count[f"out{i}"]) for i in range(2)]

    def mk(ename):
        def f(h):
            S.emit_engine(ename, h, sems)
            if ename == "pool":
                for sname, v in final_out:
                    h.wait_ge(sems[sname], v)
        return f

    with nc.allow_low_precision(reason="bf16 matmul operands by design; fp32 accumulation"), nc.Block() as block:
        block.tensor(mk("pe"))
        block.scalar(mk("act"))
        block.vector(mk("dve"))
        block.gpsimd(mk("pool"))
        block.sync(mk("sp"))
    return nc


def _bucket(d):
    d = np.maximum(d, 0)
    dd = np.maximum(d, 1).astype(np.float32)
    large = 16 + (np.log(dd / np.float32(16)) / np.float32(math.log(128 / 16)) * np.float32(16)).astype(np.int32)
    large = np.minimum(large, 31)
    return np.where(d < 16, d, large)


def _host_layout(w_in, rel_bias, mla_q_norm, w_uq, mla_kv_norm, w_uk, w_uv, w_proj_a, w_proj_b, w_out,
                 norm_attn, norm_mlp, w_mlp_up, w_mlp_down, norm_final):
    f32 = np.float32
    w_in, w_uq, w_uk, w_uv = w_in[0], w_uq[0], w_uk[0], w_uv[0]
    wpa, wpb, wo, wup, wdn = w_proj_a[0], w_proj_b[0], w_out[0], w_mlp_up[0], w_mlp_down[0]
    ga, gm, gq, gkv = norm_attn[0], norm_mlp[0], mla_q_norm[0], mla_kv_norm[0]
    wsrc = np.zeros((NSLAB, 128, 1024), f32)

    def fm(s, Wm, f, nk, g=None, slot0=0, k0=0):
        for kt in range(nk):
            wsrc[s, :, (slot0 + kt) * 128:(slot0 + kt + 1) * 128] = Wm[(k0 + kt) * 128:(k0 + kt + 1) * 128, f * 128:(f + 1) * 128]

    for f in range(4):
        fm(SL_Q + f, w_in[:, 0:512], f, 8, ga)
        fm(SL_K + f, w_in[:, 512:1024], f, 8, ga)
    for j in range(4):
        for kk in range(2):
            kt = 2 * j + kk
            wsrc[SL_V + j, :, kk * 512:(kk + 1) * 512] = w_in[kt * 128:(kt + 1) * 128, 1024:1536]
    for f in range(3):
        fm(SL_CQ + f, w_in[:, 1536:1920], f, 8, ga)
    for f in range(2):
        fm(SL_CKV + f, w_in[:, 1920:2176], f, 8, ga)
    kr = w_in[:, 2176:2208]
    krs = np.concatenate([kr[:, 16:32], kr[:, 0:16]], axis=1)
    for kt in range(8):
        wsrc[SL_KR, :, kt * 128 + 64:kt * 128 + 96] = kr[kt * 128:(kt + 1) * 128]
        wsrc[SL_KR + 1, :, kt * 128 + 64:kt * 128 + 96] = krs[kt * 128:(kt + 1) * 128]
    for kt in range(2):
        wsrc[SL_UV, :, kt * 512:(kt + 1) * 512] = w_uv[kt * 128:(kt + 1) * 128]
    for h in range(8):
        s = SL_HEAD + h
        qh = w_uq[:, h * 96:(h + 1) * 96]
        qsw = np.concatenate([qh[:, 80:96], qh[:, 64:80]], axis=1)
        for kt in range(3):
            wsrc[s, :, kt * 128:kt * 128 + 96] = qh[kt * 128:(kt + 1) * 128]
            wsrc[s, :, (3 + kt) * 128 + 64:(3 + kt) * 128 + 96] = qsw[kt * 128:(kt + 1) * 128]
        for kt in range(2):
            wsrc[s, :, (6 + kt) * 128:(6 + kt) * 128 + 64] = w_uk[kt * 128:(kt + 1) * 128, h * 64:(h + 1) * 64]
    for f in range(16):
        fm(SL_G + f, w_in[:, 2208:4256], f, 8, ga)
    for f in range(8):
        fm(SL_P + f, wpa, f, 4)
        fm(SL_P + f, wpb, f, 4, slot0=4)
        fm(SL_O + f, wo, f, 8)
        for jd in range(4):
            fm(SL_DN + f * 4 + jd, wdn, f, 8, k0=8 * jd)
    for ff in range(32):
        fm(SL_UP + ff, wup, ff, 8, gm)
    c_gains = np.zeros((128, 32), f32)
    c_gains[:, 0:8] = ga.reshape(8, 128).T
    c_gains[:, 8:16] = gm.reshape(8, 128).T
    c_gains[:, 16:19] = gq.reshape(3, 128).T
    c_gains[:, 19:21] = gkv.reshape(2, 128).T
    c_gains[:, 21:29] = norm_final.reshape(8, 128).T

    p = np.arange(128)[:, None]
    c_tri = np.where(p > np.arange(128)[None, :], NEG, 0.0).astype(f32)
    c_E = (np.arange(SEQ)[None, :] // 256 == np.arange(16)[:, None]).astype(f32)
    jj = np.arange(256)[None, :]
    c_caus = np.where(jj < p, NEG, 0.0).astype(f32)
    bk = _bucket(np.maximum(jj - p, 0))
    c_Tm = np.ascontiguousarray(np.transpose(rel_bias[bk], (0, 2, 1)).reshape(128, 8 * 256)).astype(f32)
    c_b31 = np.ascontiguousarray(np.broadcast_to(rel_bias[31][None, :], (128, 8))).astype(f32)
    half = 16
    inv = (10000.0 ** (-np.arange(half, dtype=np.float32) / half)).astype(np.float32)
    ang = np.arange(SEQ, dtype=np.float32)[None, :] * inv[:, None]
    cosv, sinv = np.cos(ang).astype(f32), np.sin(ang).astype(f32)
    c_cos = np.concatenate([cosv, cosv], 0)
    c_sin = np.concatenate([-sinv, sinv], 0)
    return dict(wsrc=wsrc, c_gains=c_gains, c_tri=c_tri, c_E=c_E, c_caus=c_caus, c_Tm=c_Tm, c_b31=c_b31,
                c_cos=c_cos, c_sin=c_sin)


_CACHE = {}


def kernel(x, w_in, rel_bias, mla_q_norm, w_uq, mla_kv_norm, w_uk, w_uv, w_proj_a, w_proj_b,
           w_out, norm_attn, norm_mlp, w_mlp_up, w_mlp_down, norm_final):
    x = np.asarray(x, np.float32)
    args = [np.asarray(a, np.float32) for a in (w_in, rel_bias, mla_q_norm, w_uq, mla_kv_norm, w_uk, w_uv, w_proj_a,
                                                w_proj_b, w_out, norm_attn, norm_mlp, w_mlp_up, w_mlp_down, norm_final)]
    shared = _host_layout(*args)
    if "nc" not in _CACHE:
        seq = build(None)
        _CACHE["nc"] = build(seq)
    nc = _CACHE["nc"]
    in_maps = []
    for b in range(8):
        m = dict(shared)
        m["xT"] = np.ascontiguousarray(x[b].T)
        in_maps.append(m)
    res = run_bass_kernel_spmd(nc, in_maps, core_ids=list(range(8)))
    out = np.stack([np.ascontiguousarray(r["outT"].T) for r in res.results], axis=0)
    return out.astype(np.float32)
```
